# Optimizing a Trainium2 kernel written in Bass

```python
import jax, jax.numpy as jnp
from jax import lax
import numpy as np

D_MODEL = 1024
BATCH = 8
SEQ = 4096
DEPTH = 1

CONV_WIDTH = D_MODEL
CONV_GROUPS = 8
CONV_KERNEL = 31
SGU_WIDTH = D_MODEL
SGU_HEADS = 8
SGU_HEAD_DIM = SGU_WIDTH // SGU_HEADS
CHUNK = 128
EPS = 1e-6

OFF_A_VAL = 0
OFF_A_GLU = OFF_A_VAL + CONV_WIDTH
OFF_A_SILU = OFF_A_GLU + CONV_WIDTH
OFF_B_U = OFF_A_SILU + CONV_WIDTH
OFF_B_V = OFF_B_U + SGU_WIDTH
OFF_B_SILU = OFF_B_V + SGU_WIDTH
OFF_G_A = OFF_B_SILU + SGU_WIDTH
OFF_G_B = OFF_G_A + D_MODEL
IN_COLS = OFF_G_B + D_MODEL

kernel_name = "hybrid_conformer_conv_gmlp_adaln"


def rmsnorm(x, g):
    x32 = x.astype(jnp.float32)
    y = x32 * lax.rsqrt(jnp.mean(x32 * x32, axis=-1, keepdims=True) + EPS)
    return y.astype(x.dtype) * g


def layernorm(x, g, b):
    x32 = x.astype(jnp.float32)
    mu = jnp.mean(x32, axis=-1, keepdims=True)
    xc = x32 - mu
    var = jnp.mean(xc * xc, axis=-1, keepdims=True)
    return (xc * lax.rsqrt(var + EPS)).astype(x.dtype) * g + b


def conformer_conv_branch(val, glu, z, conv_w, conv_b, ln_g, ln_b, w_out):
    a = val * jax.nn.sigmoid(glu)
    kern = conv_w.reshape(CONV_KERNEL, 1, CONV_WIDTH)
    y = lax.conv_general_dilated(
        a, kern, window_strides=(1,), padding=[(CONV_KERNEL - 1, 0)],
        dimension_numbers=("NWC", "WIO", "NWC"),
        feature_group_count=CONV_WIDTH) + conv_b
    y = jax.nn.silu(layernorm(y, ln_g, ln_b))
    y = y * jax.nn.silu(z)
    return y @ w_out


def sgu_branch(u, v, z, ln_g, ln_b, w_s, b_s, w_out):
    bsz, seq, _ = u.shape
    u = jax.nn.gelu(u, approximate=False)
    v = layernorm(jax.nn.gelu(v, approximate=False), ln_g, ln_b)
    vc = v.reshape(bsz, seq // CHUNK, CHUNK, SGU_HEADS, SGU_HEAD_DIM)
    causal = jnp.tril(jnp.ones((CHUNK, CHUNK), dtype=bool))
    ws = jnp.where(causal[None], w_s, jnp.zeros((), w_s.dtype))
    s = jnp.einsum("hts,bcshd->bcthd", ws, vc) + b_s.T[:, :, None]
    s = s.reshape(bsz, seq, SGU_WIDTH)
    y = u * s * jax.nn.silu(z)
    return y @ w_out


def setup_inputs(seed: int = 0) -> dict:
    key = jax.random.key(seed)
    ks = jax.random.split(key, 20)
    f32 = jnp.float32
    n = lambda k, shape, s: (jax.random.normal(k, shape, f32) * s)
    x = jax.random.normal(ks[0], (BATCH, SEQ, D_MODEL), f32)
    c = jax.random.normal(ks[1], (BATCH, D_MODEL), f32)
    w_ada = n(ks[2], (DEPTH, D_MODEL, 3 * D_MODEL), 0.3 * D_MODEL ** -0.5)
    b_ada = n(ks[3], (DEPTH, 3 * D_MODEL), 0.02)
    g_pre = 1.0 + n(ks[4], (DEPTH, D_MODEL), 0.02)
    w_in = n(ks[5], (DEPTH, D_MODEL, IN_COLS), D_MODEL ** -0.5)
    conv_w = n(ks[6], (DEPTH, CONV_KERNEL, CONV_WIDTH), CONV_KERNEL ** -0.5)
    conv_b = n(ks[7], (DEPTH, CONV_WIDTH), 0.02)
    conv_ln_g = 1.0 + n(ks[8], (DEPTH, CONV_WIDTH), 0.02)
    conv_ln_b = n(ks[9], (DEPTH, CONV_WIDTH), 0.02)
    w_conv_out = n(ks[10], (DEPTH, CONV_WIDTH, D_MODEL), CONV_WIDTH ** -0.5)
    sgu_ln_g = 1.0 + n(ks[11], (DEPTH, SGU_WIDTH), 0.02)
    sgu_ln_b = n(ks[12], (DEPTH, SGU_WIDTH), 0.02)
    w_sgu = n(ks[13], (DEPTH, SGU_HEADS, CHUNK, CHUNK), 0.5 * CHUNK ** -0.5)
    b_sgu = 1.0 + n(ks[14], (DEPTH, SGU_HEADS, CHUNK), 0.02)
    w_sgu_out = n(ks[15], (DEPTH, SGU_WIDTH, D_MODEL), SGU_WIDTH ** -0.5)
    w_o = n(ks[16], (DEPTH, D_MODEL, D_MODEL), D_MODEL ** -0.5)
    g_final = 1.0 + n(ks[17], (D_MODEL,), 0.02)
    return {"x": x, "c": c, "w_ada": w_ada, "b_ada": b_ada, "g_pre": g_pre, "w_in": w_in,
            "conv_w": conv_w, "conv_b": conv_b, "conv_ln_g": conv_ln_g, "conv_ln_b": conv_ln_b,
            "w_conv_out": w_conv_out, "sgu_ln_g": sgu_ln_g, "sgu_ln_b": sgu_ln_b,
            "w_sgu": w_sgu, "b_sgu": b_sgu, "w_sgu_out": w_sgu_out, "w_o": w_o,
            "g_final": g_final}


def reference(x, c, w_ada, b_ada, g_pre, w_in, conv_w, conv_b, conv_ln_g, conv_ln_b,
              w_conv_out, sgu_ln_g, sgu_ln_b, w_sgu, b_sgu, w_sgu_out, w_o, g_final):
    for l in range(DEPTH):
        mod = c @ w_ada[l] + b_ada[l]
        shift, scale, gate = jnp.split(mod, 3, axis=-1)
        h = rmsnorm(x, g_pre[l]) * (1.0 + scale[:, None, :]) + shift[:, None, :]
        p = h @ w_in[l]
        y_a = conformer_conv_branch(
            p[..., OFF_A_VAL:OFF_A_GLU], p[..., OFF_A_GLU:OFF_A_SILU], p[..., OFF_A_SILU:OFF_B_U],
            conv_w[l], conv_b[l], conv_ln_g[l], conv_ln_b[l], w_conv_out[l])
        y_b = sgu_branch(
            p[..., OFF_B_U:OFF_B_V], p[..., OFF_B_V:OFF_B_SILU], p[..., OFF_B_SILU:OFF_G_A],
            sgu_ln_g[l], sgu_ln_b[l], w_sgu[l], b_sgu[l], w_sgu_out[l])
        merged = (jax.nn.sigmoid(p[..., OFF_G_A:OFF_G_B]) * y_a
                  + jax.nn.sigmoid(p[..., OFF_G_B:IN_COLS]) * y_b)
        x = x + gate[:, None, :] * (merged @ w_o[l])
    return rmsnorm(x, g_final)
```

```python
import numpy as np
import concourse.bass as bass
import concourse.mybir as mybir
from concourse.bass_utils import run_bass_kernel_spmd
from contextlib import ExitStack

F32 = mybir.dt.float32
BF16 = mybir.dt.bfloat16
ALU = mybir.AluOpType
AF = mybir.ActivationFunctionType

ENG = ("pe", "act", "dve", "pool", "sp")

D = 1024
S = 4096
T = 512
NT = S // T
EPS = 1e-6
AH = 32
RW = 540
RH = 28


class Prog:
    def __init__(self, nc, stack):
        self.nc = nc
        self.items = {e: [] for e in ENG}
        self.esem = {e: stack.enter_context(nc.semaphore("s_" + e)) for e in ENG if e != "sp"}
        self.ecnt = {e: 0 for e in ENG}
        self.seen = {e: {} for e in ENG}
        self.dsem = {}
        self.stack = stack
        self.buf = {}

    def _st(self, b):
        st = self.buf.get(b)
        if st is None:
            st = {"w": None, "r": {}}
            self.buf[b] = st
        return st

    def _deps(self, e, reads, writes):
        deps = []
        for b in reads:
            st = self._st(b)
            if st["w"] is not None:
                deps.append((st["w"], True))
        for b in writes:
            st = self._st(b)
            if st["w"] is not None:
                deps.append((st["w"], False))
            for t in st["r"].values():
                deps.append((t, False))
        need = {}
        for (sem, val, src), raw in deps:
            if src == e:
                if e == "pe":
                    continue
                if e in ("act", "dve", "pool") and not raw:
                    continue
            k = id(sem)
            if val > need.get(k, (None, 0))[1]:
                need[k] = (sem, val)
        for k, (sem, val) in need.items():
            if self.seen[e].get(k, 0) >= val:
                continue
            self.seen[e][k] = val
            self.items[e].append(("wait", sem, val))

    def _mark(self, tok, reads, writes):
        for b in reads:
            st = self._st(b)
            k = id(tok[0])
            old = st["r"].get(k)
            if old is None or old[1] < tok[1]:
                st["r"][k] = tok
        for b in writes:
            st = self._st(b)
            st["w"] = tok
            st["r"] = {}

    def op(self, e, fn, reads=(), writes=()):
        self._deps(e, reads, writes)
        self.ecnt[e] += 1
        tok = (self.esem[e], self.ecnt[e], e)
        self.items[e].append(("op", fn, self.esem[e], 1))
        self._mark(tok, reads, writes)
        return tok

    def dma(self, q, out, in_, reads=(), writes=(), key=None, **kw):
        self._deps(q, reads, writes)
        if key not in self.dsem:
            self.dsem[key] = [self.stack.enter_context(self.nc.semaphore("d_" + str(key))), 0]
        ent = self.dsem[key]
        ent[1] += 16
        tok = (ent[0], ent[1], "dma:" + str(key))

        def fn(eng, out=out, in_=in_, kw=kw):
            return eng.dma_start(out=out, in_=in_, **kw)

        self.items[q].append(("op", fn, ent[0], 16))
        self._mark(tok, reads, writes)
        return tok

    def retag(self, names, tok):
        for b in names:
            self._st(b)["w"] = tok

    def wait_all(self, e):
        need = {}
        for st in self.buf.values():
            toks = list(st["r"].values())
            if st["w"] is not None:
                toks.append(st["w"])
            for sem, val, src in toks:
                k = id(sem)
                if val > need.get(k, (None, 0))[1]:
                    need[k] = (sem, val)
        for k, (sem, val) in need.items():
            if self.seen[e].get(k, 0) >= val:
                continue
            self.seen[e][k] = val
            self.items[e].append(("wait", sem, val))

    def emit(self):
        nc = self.nc
        items = self.items

        def replay(name, eng):
            for it in items[name]:
                if it[0] == "wait":
                    eng.wait_ge(it[1], it[2])
                else:
                    ins = it[1](eng)
                    ins.then_inc(it[2], it[3])

        with nc.Block() as block:
            @block.tensor
            def _(pe):
                replay("pe", pe)

            @block.scalar
            def _(act):
                replay("act", act)

            @block.vector
            def _(dve):
                replay("dve", dve)

            @block.gpsimd
            def _(pool):
                replay("pool", pool)

            @block.sync
            def _(sp):
                replay("sp", sp)


C_C, C_GPRE, C_CONVB, C_CLG, C_CLB, C_SLG, C_SLB = range(7)


def build_nc(NT=NT, debug=False):
    nc = bass.Bass("TRN2", target_bir_lowering=False)
    dbg_n = [0]
    dt_in = lambda name, shape: nc.dram_tensor(name, shape, F32, kind="ExternalInput").ap()
    x_d = dt_in("x", [S, D])
    wada_d = dt_in("w_ada", [D, 3 * D])
    bada_d = dt_in("b_ada_bc", [128, 3 * D])
    win_d = dt_in("w_in", [D, 8 * D])
    cwp_d = dt_in("cwp", [128, 256])
    cols_d = dt_in("cols", [128, 56])
    wco_d = dt_in("w_conv_out", [D, D])
    wso_d = dt_in("w_sgu_out", [D, D])
    wo_d = dt_in("w_o", [D, D])
    wsT_d = dt_in("wsT", [128, 1024])
    bs_d = dt_in("bs_bc", [128, 1024])
    gfin_d = dt_in("gfin_bc", [128, 1024])
    cst_d = dt_in("consts", [128, 288])
    out_d = nc.dram_tensor("out", [S, D], F32, kind="ExternalOutput").ap()
    scr_win = nc.dram_tensor("scr_win", [D, 8 * D], BF16).ap()
    scr_wco = nc.dram_tensor("scr_wco", [D, D], BF16).ap()
    scr_wso = nc.dram_tensor("scr_wso", [D, D], BF16).ap()
    scr_wo = nc.dram_tensor("scr_wo", [D, D], BF16).ap()

    with ExitStack() as stk:
        P = Prog(nc, stk)
        sb = lambda name, shape, dtype: nc.alloc_sbuf_tensor("sb_" + name, shape, dtype)
        cst = sb("cst", [128, 288], F32)
        ident32 = cst[:, 0:128]
        maskT = cst[:, 128:256]
        Emask = cst[:, 256:288]
        ones_bf = sb("ones_bf", [128, 128], BF16)
        ones32 = sb("ones32", [128, 128], F32)
        neghalf = sb("neghalf", [128, 512], F32)
        cols = sb("cols", [128, 56], F32)
        a_col = sb("a_col", [128, 8], F32)
        shift_col = sb("shift_col", [128, 8], F32)
        cwp = sb("cwp", [128, 256], F32)
        Wk = sb("Wk", [128, 8, 4, 8, 32], BF16)
        wsTm = sb("wsTm", [128, 8, 128], BF16)
        Cb = sb("Cb", [128, 8, 128], F32)
        gfin = sb("gfin", [128, 1024], F32)
        big = [None, None] + [sb("big%d" % i, [128, 1024], F32) for i in range(2, 10)]
        vnb = sb("vn", [128, 4, 1024], BF16)
        xf = big[2:4]
        xn = big[4:8]
        v32 = big[8:10]
        hT = [sb("hT%d" % i, [128, 8, T], BF16) for i in range(2)]
        aT = sb("aT", [128, 8, AH + T], BF16)
        Rb = [sb("R%d" % i, [128, 2, 4, RW], BF16) for i in range(2)]
        ys = sb("ys", [128, 8, T], BF16)
        zs = sb("zs", [128, 8, T], BF16)
        us = sb("us", [128, 8, T], BF16)
        ms = sb("ms", [128, 8, T], BF16)
        sg = [sb("sg%d" % i, [128, T], BF16) for i in range(4)]
        ysq = [sb("ysq%d" % i, [128, T], BF16) for i in range(2)]
        mbt = [sb("mbt%d" % i, [128, T], BF16) for i in range(2)]
        t32a = [sb("t32a%d" % i, [128, T], F32) for i in range(2)]
        t32b = [sb("t32b%d" % i, [128, T], F32) for i in range(2)]
        meanB = sb("meanB", [128, T], F32)
        mbt32 = sb("mbt32", [128, T], F32)
        msqB = sb("msqB", [128, T], F32)
        rstdB = sb("rstdB", [128, T], F32)
        wb = [sb("wb%d" % i, [128, 8, T], BF16) for i in range(4)]
        gateb = sb("gateb", [128, 1024], F32)
        NSM = 10
        st6 = [sb("st6_%d" % i, [128, 12], F32) for i in range(NSM)]
        mv = [sb("mv_%d" % i, [128, 2], F32) for i in range(NSM)]
        tmp1 = [sb("tmp1_%d" % i, [128, 1], F32) for i in range(NSM)]
        rstd1 = [sb("rstd1_%d" % i, [128, 1], F32) for i in range(NSM)]
        banks = [nc.alloc_psum_tensor("ps%d" % i, [128, 512], F32) for i in range(8)]
        NRING = 6
        ring = [0]

        def dump(tag, t, names):
            if not debug:
                return
            shp = list(t.shape)
            flat = [shp[0], int(np.prod(shp[1:]))]
            d = nc.dram_tensor("dbg_" + tag, flat, t.dtype, kind="ExternalOutput").ap()
            a = t.ap()
            if len(shp) == 3:
                a = a.rearrange("p a b -> p (a b)")
            elif len(shp) == 5:
                a = a.rearrange("p a b c d -> p (a b c d)")
            P.dma("sp", d, a, reads=names, writes=["dbgout_" + tag], key="dbg")

        def col(k, j):
            return cols[:, k * 8 + j:k * 8 + j + 1]

        def nextbank():
            b = ring[0] % NRING
            ring[0] += 1
            return b

        def pe_job(fn, reads, b=None):
            if b is None:
                b = nextbank()
            P.op("pe", lambda pe, fn=fn, b=b: fn(pe, banks[b]), reads=reads, writes=["ps%d" % b])
            return b

        P.dma("sp", cst.ap(), cst_d, writes=["cst"], key="c0")
        P.dma("sp", cols.ap(), cols_d, writes=["cols"], key="c0")
        P.dma("sp", cwp.ap(), cwp_d, writes=["cwp"], key="c0")
        tokc = P.dma("sp", gfin.ap(), gfin_d, writes=["gfin"], key="c0")
        P.retag(["cst", "cols", "cwp", "gfin"], tokc)
        P.op("dve", lambda e: e.memset(ones_bf.ap(), 1.0), writes=["ones_bf"])
        P.op("dve", lambda e: e.memset(ones32.ap(), 1.0), writes=["ones32"])
        P.op("dve", lambda e: e.memset(neghalf.ap(), -0.5), writes=["neghalf"])
        P.op("dve", lambda e: e.memset(aT.ap(), 0.0), writes=["aT%d" % j for j in range(8)] + ["aTh%d" % j for j in range(8)])

        piece_src = {}
        cast_done = set()
        for cb in range(16):
            piece_src[("win", cb)] = (scr_win, cb * T, ["scr_win%d" % cb])
        for h in range(2):
            piece_src[("wco", h)] = (scr_wco, h * T, ["scr_wco"])
            piece_src[("wso", h)] = (scr_wso, h * T, ["scr_wso"])
            piece_src[("wo", h)] = (scr_wo, h * T, ["scr_wo"])

        def ensure_cast(kind):
            k0 = kind[0]
            tag = kind if k0 == "win" else (k0,)
            if tag in cast_done:
                return
            cast_done.add(tag)
            if k0 == "win":
                cb = kind[1]
                P.dma("pool", scr_win[:, cb * T:(cb + 1) * T], win_d[:, cb * T:(cb + 1) * T],
                      reads=([] if cb in (2, 0) else ["a_col"]),
                      writes=["scr_win%d" % cb], key="cast%d" % (len(cast_done) % 6))
            elif k0 == "wco":
                P.dma("pool", scr_wco, wco_d, reads=["a_col"], writes=["scr_wco"], key="cast%d" % (len(cast_done) % 6))
            elif k0 == "wso":
                P.dma("pool", scr_wso, wso_d, reads=["a_col"], writes=["scr_wso"], key="cast%d" % (len(cast_done) % 6))
            else:
                P.dma("pool", scr_wo, wo_d, reads=["a_col"], writes=["scr_wo"], key="cast%d" % (len(cast_done) % 6))

        for kind in (("win", 2), ("win", 0)):
            ensure_cast(kind)

        tile_pieces = [("win", 2), ("win", 0), ("win", 3), ("win", 1), ("win", 4), ("win", 5),
                       ("win", 10), ("win", 11), ("win", 6), ("win", 7),
                       ("win", 8), ("win", 9),
                       ("win", 12), ("wco", 0), ("win", 13), ("wco", 1),
                       ("win", 14), ("wso", 0), ("win", 15), ("wso", 1), ("wo", 0), ("wo", 1)]
        all_pieces = []
        for i in range(NT):
            all_pieces += tile_pieces
        issued = [0]
        opened = [0]

        def issue_upto(m):
            while issued[0] < min(m, len(all_pieces)):
                n = issued[0]
                for la in range(n, min(n + 5, len(all_pieces))):
                    ensure_cast(all_pieces[la])
                src, c0, snames = piece_src[all_pieces[n]]
                k = n % 4
                P.dma("sp", wb[k].ap(), src.rearrange("(kc p) n -> p kc n", p=128)[:, :, c0:c0 + T],
                      reads=snames, writes=["wb%d" % k], key="wb%d" % k)
                issued[0] += 1

        def open_piece(kind):
            n = opened[0]
            assert all_pieces[n] == kind, (all_pieces[n], kind)
            opened[0] += 1
            issue_upto(n + 1)
            return wb[n % 4], "wb%d" % (n % 4), n

        def close_piece(n):
            issue_upto(n + 4 + 1)

        def small_stats(k, src, sname, rms):
            P.op("dve", lambda e: e.bn_stats(out=st6[k][:, 0:6], in_=src[:, 0:512]), reads=[sname], writes=["st6a_%d" % k])
            P.op("dve", lambda e: e.bn_stats(out=st6[k][:, 6:12], in_=src[:, 512:1024]), reads=[sname], writes=["st6b_%d" % k])
            P.op("dve", lambda e: e.bn_aggr(out=mv[k][:, :], in_=st6[k][:, :]),
                 reads=["st6a_%d" % k, "st6b_%d" % k], writes=["mv_%d" % k])
            if rms:
                P.op("dve", lambda e: e.scalar_tensor_tensor(
                    out=tmp1[k][:, :], in0=mv[k][:, 0:1], scalar=mv[k][:, 0:1], in1=mv[k][:, 1:2],
                    op0=ALU.mult, op1=ALU.add), reads=["mv_%d" % k], writes=["tmp1_%d" % k])
                P.op("dve", lambda e: e.tensor_scalar(out=tmp1[k][:, :], in0=tmp1[k][:, :], scalar1=EPS, scalar2=None,
                                                      op0=ALU.add), reads=["tmp1_%d" % k], writes=["tmp1_%d" % k])
            else:
                P.op("dve", lambda e: e.tensor_scalar(out=tmp1[k][:, :], in0=mv[k][:, 1:2], scalar1=EPS, scalar2=None,
                                                      op0=ALU.add), reads=["mv_%d" % k], writes=["tmp1_%d" % k])
            P.op("pool", lambda e: e.tensor_tensor(out=rstd1[k][:, :], in0=tmp1[k][:, :], in1=neghalf[:, 0:1],
                                                   op=ALU.pow),
                 reads=["tmp1_%d" % k, "neghalf"], writes=["rstd1_%d" % k])

        def xprep_a(i, u):
            row0 = i * T + u * 128
            nm = "big%d" % (4 + u)
            P.dma("sp", xn[u].ap(), x_d[row0:row0 + 128, :], writes=[nm], key="ld_" + nm)
            small_stats(6 + u, xn[u], nm, True)

        def xprep_b(i, u):
            k = 6 + u
            nm = "big%d" % (4 + u)
            P.op("dve", lambda e: e.tensor_scalar(out=xn[u][:, :], in0=xn[u][:, :], scalar1=rstd1[k][:, 0:1],
                                                  scalar2=None, op0=ALU.mult),
                 reads=[nm, "rstd1_%d" % k], writes=[nm])

        def xprep_transposes(i):
            par = i % 2
            for kc in range(8):
                def tjob(pe, bank, kc=kc):
                    last = None
                    for u in range(4):
                        last = pe.transpose(out=bank[:, u * 128:(u + 1) * 128],
                                            in_=xn[u][:, kc * 128:(kc + 1) * 128], identity=ident32)
                    return last
                b = pe_job(tjob, reads=["big4", "big5", "big6", "big7", "cst"])
                P.op("act", lambda e, b=b, kc=kc: e.activation(
                    out=hT[par][:, kc, :], in_=banks[b][:, :], func=AF.Identity,
                    bias=shift_col[:, kc:kc + 1], scale=a_col[:, kc:kc + 1]),
                    reads=["ps%d" % b, "shift_col", "a_col"], writes=["hT%d_%d" % (par, kc)])

        def hT_names(par):
            return ["hT%d_%d" % (par, kc) for kc in range(8)]

        def mm_cm(wbuf, lc, par):
            def fn(pe, bank):
                last = None
                for kc in range(8):
                    last = pe.matmul(bank[:, :], wbuf[:, kc, lc * 128:(lc + 1) * 128], hT[par][:, kc, :],
                                     start=(kc == 0), stop=(kc == 7))
                return last
            return fn

        for u in range(4):
            xprep_a(0, u)

        t512 = [(t32a[0], "t32a0"), (t32a[1], "t32a1"), (t32b[0], "t32b0"), (t32b[1], "t32b1")]
        tb = [0]

        def next_t512():
            r = t512[tb[0] % len(t512)]
            tb[0] += 1
            return r

        t512b = [(meanB, "meanB"), (msqB, "msqB"), (rstdB, "rstdB"), (mbt32, "mbt32")]
        tb2 = [0]

        def next_t512b():
            r = t512b[tb2[0] % len(t512b)]
            tb2[0] += 1
            return r

        def ada_block(acc, aname, cb, queue="sp", split=False):
            for kc in range(8):
                for half in range(2):
                    sl = slice(half * 512, (half + 1) * 512)
                    buf, bname = next_t512b() if split else next_t512()
                    c0 = cb * 1024 + half * 512
                    P.dma(queue, buf.ap(), wada_d[kc * 128:(kc + 1) * 128, c0:c0 + 512],
                          writes=[bname], key="ld_" + bname)
                    if split:
                        if kc == 0:
                            P.op("act", lambda e, buf=buf, sl=sl: e.activation(
                                out=acc[:, sl], in_=buf[:, :], func=AF.Identity, scale=col(C_C, 0)),
                                reads=[bname, "cols"], writes=[aname])
                        else:
                            P.op("act", lambda e, buf=buf, kc=kc: e.activation(
                                out=buf[:, :], in_=buf[:, :], func=AF.Identity, scale=col(C_C, kc)),
                                reads=[bname, "cols"], writes=[bname])
                            P.op("pool", lambda e, buf=buf, sl=sl: e.tensor_tensor(
                                out=acc[:, sl], in0=acc[:, sl], in1=buf[:, :], op=ALU.add),
                                reads=[bname, aname], writes=[aname])
                    elif kc == 0:
                        P.op("dve", lambda e, buf=buf, sl=sl: e.tensor_scalar(
                            out=acc[:, sl], in0=buf[:, :], scalar1=col(C_C, 0), scalar2=None, op0=ALU.mult),
                            reads=[bname, "cols"], writes=[aname])
                    else:
                        P.op("dve", lambda e, buf=buf, sl=sl, kc=kc: e.scalar_tensor_tensor(
                            out=acc[:, sl], in0=buf[:, :], scalar=col(C_C, kc), in1=acc[:, sl],
                            op0=ALU.mult, op1=ALU.add),
                            reads=[bname, "cols", aname], writes=[aname])
            for half in range(2):
                sl = slice(half * 512, (half + 1) * 512)
                buf, bname = next_t512()
                c0 = cb * 1024 + half * 512
                P.dma(queue, buf.ap(), bada_d[:, c0:c0 + 512], writes=[bname], key="ld_" + bname)
                b = pe_job(lambda pe, bank, sl=sl: pe.matmul(bank[:, :], ones32[:, :], acc[:, sl],
                                                             start=True, stop=True),
                           reads=["ones32", aname])
                P.op("dve", lambda e, b=b, sl=sl, buf=buf: e.tensor_tensor(
                    out=acc[:, sl], in0=banks[b][:, :], in1=buf[:, :], op=ALU.add),
                    reads=["ps%d" % b, bname], writes=[aname])

        accs = [(v32[0], "big8"), (v32[1], "big9")]
        for kc in range(8):
            for cb in range(2):
                acc, aname = accs[cb]
                for half in range(2):
                    sl = slice(half * 512, (half + 1) * 512)
                    buf, bname = next_t512()
                    c0 = cb * 1024 + half * 512
                    P.dma("sp", buf.ap(), wada_d[kc * 128:(kc + 1) * 128, c0:c0 + 512], writes=[bname], key="ld_" + bname)
                    if kc == 0:
                        P.op("dve", lambda e, buf=buf, sl=sl, acc=acc: e.tensor_scalar(
                            out=acc[:, sl], in0=buf[:, :], scalar1=col(C_C, 0), scalar2=None, op0=ALU.mult),
                            reads=[bname, "cols"], writes=[aname])
                    else:
                        P.op("dve", lambda e, buf=buf, sl=sl, kc=kc, acc=acc: e.scalar_tensor_tensor(
                            out=acc[:, sl], in0=buf[:, :], scalar=col(C_C, kc), in1=acc[:, sl],
                            op0=ALU.mult, op1=ALU.add),
                            reads=[bname, "cols", aname], writes=[aname])
        for cb in range(2):
            acc, aname = accs[cb]
            for half in range(2):
                sl = slice(half * 512, (half + 1) * 512)
                buf, bname = next_t512()
                c0 = cb * 1024 + half * 512
                P.dma("sp", buf.ap(), bada_d[:, c0:c0 + 512], writes=[bname], key="ld_" + bname)
                b = pe_job(lambda pe, bank, sl=sl, acc=acc: pe.matmul(bank[:, :], ones32[:, :], acc[:, sl],
                                                                      start=True, stop=True),
                           reads=["ones32", aname])
                P.op("dve", lambda e, b=b, sl=sl, buf=buf, acc=acc: e.tensor_tensor(
                    out=acc[:, sl], in0=banks[b][:, :], in1=buf[:, :], op=ALU.add),
                    reads=["ps%d" % b, bname], writes=[aname])
        for kc in range(8):
            for half in range(2):
                sl = slice(half * 512, (half + 1) * 512)
                buf, bname = next_t512b()
                c0 = 2 * 1024 + half * 512
                P.dma("sp", buf.ap(), wada_d[kc * 128:(kc + 1) * 128, c0:c0 + 512], writes=[bname], key="ld_" + bname)
                if kc == 0:
                    P.op("act", lambda e, buf=buf, sl=sl: e.activation(
                        out=gateb[:, sl], in_=buf[:, :], func=AF.Identity, scale=col(C_C, 0)),
                        reads=[bname, "cols"], writes=["gateb"])
                else:
                    P.op("act", lambda e, buf=buf, kc=kc: e.activation(
                        out=buf[:, :], in_=buf[:, :], func=AF.Identity, scale=col(C_C, kc)),
                        reads=[bname, "cols"], writes=[bname])
                    P.op("pool", lambda e, buf=buf, sl=sl: e.tensor_tensor(
                        out=gateb[:, sl], in0=gateb[:, sl], in1=buf[:, :], op=ALU.add),
                        reads=[bname, "gateb"], writes=["gateb"])
        P.dma("sp", big[2].ap(), bada_d[:, 2048:3072], writes=["big2"], key="ld_big2")

        def gate_finalize():
            for half in range(2):
                sl = slice(half * 512, (half + 1) * 512)
                b = pe_job(lambda pe, bank, sl=sl: pe.matmul(bank[:, :], ones32[:, :], gateb[:, sl],
                                                             start=True, stop=True),
                           reads=["ones32", "gateb"])
                P.op("dve", lambda e, b=b, sl=sl: e.tensor_tensor(
                    out=gateb[:, sl], in0=banks[b][:, :], in1=big[2][:, sl], op=ALU.add),
                    reads=["ps%d" % b, "big2"], writes=["gateb"])

        for u in range(4):
            xprep_b(0, u)
        cwp5 = cwp.ap().rearrange("p (j g q) -> p j g q", j=8, g=4)
        for j in range(8):
            for g in range(4):
                P.op("pool", lambda e, j=j, g=g: e.tensor_tensor(
                    out=Wk[:, j, g, :, :],
                    in0=Emask.unsqueeze(1).to_broadcast([128, 8, 32]),
                    in1=cwp5[:, j, g, :].unsqueeze(2).to_broadcast([128, 8, 32]),
                    op=ALU.mult), reads=["cst", "cwp"], writes=["Wk"])


        def colx(pe, bank):
            last = None
            for idx in range(16):
                src = v32[idx // 8]
                kc = idx % 8
                last = pe.matmul(bank[:, 2 * idx:2 * idx + 2], src[:, kc * 128:(kc + 1) * 128], ident32[:, 0:2],
                                 start=True, stop=True)
            return last
        b = pe_job(colx, reads=["big8", "big9", "cst"])
        P.op("dve", lambda e, b=b: e.tensor_copy(out=shift_col[:, :], in_=banks[b][:, 0:16:2]),
             reads=["ps%d" % b], writes=["shift_col"])
        P.op("dve", lambda e, b=b: e.scalar_tensor_tensor(
            out=a_col[:, :], in0=banks[b][:, 16:32:2], scalar=1.0, in1=cols[:, C_GPRE * 8:C_GPRE * 8 + 8],
            op0=ALU.add, op1=ALU.mult), reads=["ps%d" % b, "cols"], writes=["a_col"])
        dump("a_col", a_col, ["a_col"]); dump("shift_col", shift_col, ["shift_col"])
        issue_upto(4)
        xprep_transposes(0)

        P.dma("sp", big[8].ap(), wsT_d, writes=["big8"], key="ld_big8")
        P.op("dve", lambda e: e.tensor_tensor(
            out=wsTm[:, :, :], in0=big[8].ap().rearrange("p (h t) -> p h t", h=8),
            in1=maskT.unsqueeze(1).to_broadcast([128, 8, 128]), op=ALU.mult),
            reads=["big8", "cst"], writes=["wsTm"])
        P.dma("sp", big[3].ap(), bs_d, writes=["big3"], key="ld_big3")
        wsflat = wsTm.ap().rearrange("p h t -> p (h t)")
        for half in range(2):
            b = pe_job(lambda pe, bank, half=half: pe.matmul(bank[:, :], ones_bf[:, :],
                                                             wsflat[:, half * 512:(half + 1) * 512],
                                                             start=True, stop=True),
                       reads=["ones_bf", "wsTm"])
            for hh in range(4):
                h = half * 4 + hh
                P.op("dve", lambda e, b=b, h=h, hh=hh: e.scalar_tensor_tensor(
                    out=Cb[:, h, :], in0=banks[b][:, hh * 128:(hh + 1) * 128], scalar=col(C_SLB, h),
                    in1=big[3][:, h * 128:(h + 1) * 128], op0=ALU.mult, op1=ALU.add),
                    reads=["ps%d" % b, "cols", "big3"], writes=["Cb"])
        dump("Cb", Cb, ["Cb"]); dump("wsTm", wsTm, ["wsTm"]); dump("Wk", Wk, ["Wk"])

        dump("hT0", hT[0], hT_names(0))
        deferred = [gate_finalize]
        for i in range(NT):
            par = i % 2
            hn = hT_names(par)
            def conv(j):
                pb = (j // 2) % 2
                jj = j % 2
                jb = j % 2
                rn = ["R%d_%d_%d" % (pb, g, s) for g in range(4) for s in range(4)]

                def cjob(pe, bank):
                    last = None
                    for q in range(8):
                        for g in range(4):
                            last = pe.matmul(bank[32 * g:32 * g + 32, :], Wk[:, j, g, q, :],
                                             Rb[pb][:, jj, g, RH - 4 * q:RH - 4 * q + T],
                                             start=(q == 0), stop=(q == 7), tile_position=(0, 32 * g))
                    return last
                b = pe_job(cjob, reads=rn + ["Wk"])
                P.op("act", lambda e, b=b: e.activation(out=ys[:, j, :], in_=banks[b][:, :], func=AF.Identity,
                                                        bias=col(C_CONVB, j), scale=1.0),
                     reads=["ps%d" % b, "cols"], writes=["ys%d" % j])
                P.op("act", lambda e, b=b: e.activation(out=ysq[jb][:, :], in_=banks[b][:, :], func=AF.Square,
                                                        bias=col(C_CONVB, j), scale=1.0),
                     reads=["ps%d" % b, "cols"], writes=["ysq%d" % jb])

            def stats(j):
                jb = j % 2

                def sjob(pe):
                    pe.matmul(banks[6][:, :], ones_bf[:, :], ys[:, j, :], start=(j == 0), stop=(j == 7))
                    return pe.matmul(banks[7][:, :], ones_bf[:, :], ysq[jb][:, :], start=(j == 0), stop=(j == 7))
                P.op("pe", sjob, reads=["ones_bf", "ys%d" % j, "ysq%d" % jb], writes=["ps6", "ps7"])

            def glu_val(j, pg, pv):
                lc = j % 4
                jb = j % 2
                b1 = pe_job(mm_cm(pg[0], lc, par), reads=[pg[1]] + hn)
                P.op("act", lambda e, b1=b1, jb=jb: e.activation(out=sg[jb][:, :], in_=banks[b1][:, :],
                                                                 func=AF.Sigmoid),
                     reads=["ps%d" % b1], writes=["sg%d" % jb])
                b2 = pe_job(mm_cm(pv[0], lc, par), reads=[pv[1]] + hn)
                P.op("dve", lambda e, b2=b2, jb=jb, j=j: e.tensor_tensor(
                    out=aT[:, j, AH:AH + T], in0=banks[b2][:, :], in1=sg[jb][:, :], op=ALU.mult),
                    reads=["ps%d" % b2, "sg%d" % jb], writes=["aT%d" % j])

            def replicas(p):
                pb = p % 2
                j0 = 2 * p
                for q, key, gs in (("sp", "R%d" % pb, (0, 1)), ("pool", "RP%d" % pb, (2, 3))):
                    rn = []
                    tok = None
                    for g in gs:
                        for s in range(4):
                            nm = "R%d_%d_%d" % (pb, g, s)
                            rn.append(nm)
                            c0 = AH - RH - s
                            tok = P.dma(q, Rb[pb][32 * s:32 * s + 32, :, g, :],
                                        aT[32 * g:32 * g + 32, j0:j0 + 2, c0:c0 + RW],
                                        reads=["aT%d" % j0, "aTh%d" % j0, "aT%d" % (j0 + 1), "aTh%d" % (j0 + 1)],
                                        writes=[nm], key=key)
                    P.retag(rn, tok)

            def zA_job(j, pz):
                b = pe_job(mm_cm(pz[0], j % 4, par), reads=[pz[1]] + hn)
                P.op("act", lambda e, b=b, j=j: e.activation(out=zs[:, j, :], in_=banks[b][:, :], func=AF.Silu),
                     reads=["ps%d" % b], writes=["zs%d" % j])

            pg = open_piece(("win", 2))
            pv = open_piece(("win", 0))
            glu_val(0, pg, pv); glu_val(1, pg, pv); replicas(0)
            for fn in deferred:
                fn()
            del deferred[:]
            glu_val(2, pg, pv); glu_val(3, pg, pv); replicas(1)
            close_piece(pg[2]); close_piece(pv[2])
            pg = open_piece(("win", 3))
            pv = open_piece(("win", 1))
            glu_val(4, pg, pv)
            conv(0); conv(1)
            glu_val(5, pg, pv); replicas(2)
            glu_val(6, pg, pv)
            stats(0); stats(1); conv(2); conv(3)
            glu_val(7, pg, pv); replicas(3)
            close_piece(pg[2]); close_piece(pv[2])
            pz = open_piece(("win", 4))
            zA_job(0, pz)
            stats(2); stats(3); conv(4); conv(5)
            zA_job(1, pz); zA_job(2, pz); zA_job(3, pz)
            close_piece(pz[2])
            pz = open_piece(("win", 5))
            zA_job(4, pz)
            stats(4); stats(5); conv(6); conv(7)
            zA_job(5, pz); zA_job(6, pz)
            stats(6); stats(7)
            zA_job(7, pz)
            close_piece(pz[2])
            for j in range(8):
                P.op("pool", lambda e, j=j: e.tensor_copy(out=aT[:, j, 0:AH], in_=aT[:, j, T:T + AH]),
                     reads=["aT%d" % j], writes=["aTh%d" % j])
            if i == 0:
                dump("aT", aT, ["aT%d" % k for k in range(8)] + ["aTh%d" % k for k in range(8)])
                dump("yconv", ys, ["ys%d" % k for k in range(8)])
            P.op("dve", lambda e: e.tensor_scalar(out=meanB[:, :], in0=banks[6][:, :], scalar1=1.0 / D, scalar2=None,
                                                  op0=ALU.mult), reads=["ps6"], writes=["meanB"])
            P.op("dve", lambda e: e.tensor_tensor(out=msqB[:, :], in0=meanB[:, :], in1=meanB[:, :], op=ALU.mult),
                 reads=["meanB"], writes=["msqB"])
            P.op("dve", lambda e: e.scalar_tensor_tensor(out=rstdB[:, :], in0=banks[7][:, :], scalar=1.0 / D,
                                                         in1=msqB[:, :], op0=ALU.mult, op1=ALU.subtract),
                 reads=["ps7", "msqB"], writes=["rstdB"])
            P.op("dve", lambda e: e.tensor_scalar(out=rstdB[:, :], in0=rstdB[:, :], scalar1=0.0, scalar2=EPS,
                                                  op0=ALU.max, op1=ALU.add), reads=["rstdB"], writes=["rstdB"])
            P.op("act", lambda e: e.activation(out=msqB[:, :], in_=rstdB[:, :], func=AF.Sqrt),
                 reads=["rstdB"], writes=["msqB"])
            P.op("dve", lambda e: e.reciprocal(out=rstdB[:, :], in_=msqB[:, :]), reads=["msqB"], writes=["rstdB"])
            for j in range(8):
                if j % 4 == 0:
                    pzb = open_piece(("win", 10 + j // 4))
                jb = j % 2
                b = pe_job(mm_cm(pzb[0], j % 4, par), reads=[pzb[1]] + hn)
                P.op("act", lambda e, b=b, j=j: e.activation(out=us[:, j, :], in_=banks[b][:, :], func=AF.Silu),
                     reads=["ps%d" % b], writes=["us%d" % j])
                if j % 4 == 3:
                    close_piece(pzb[2])
                P.op("pool", lambda e, j=j, jb=jb: e.tensor_tensor(out=t32a[jb][:, :], in0=ys[:, j, :], in1=meanB[:, :],
                                                                   op=ALU.subtract),
                     reads=["ys%d" % j, "meanB"], writes=["t32a%d" % jb])
                P.op("dve", lambda e, jb=jb: e.tensor_tensor(out=t32b[jb][:, :], in0=t32a[jb][:, :], in1=rstdB[:, :],
                                                             op=ALU.mult),
                     reads=["t32a%d" % jb, "rstdB"], writes=["t32b%d" % jb])
                P.op("act", lambda e, j=j, jb=jb: e.activation(out=ys[:, j, :], in_=t32b[jb][:, :], func=AF.Silu,
                                                               bias=col(C_CLB, j), scale=col(C_CLG, j)),
                     reads=["t32b%d" % jb, "cols"], writes=["ys%d" % j])
                P.op("dve", lambda e, j=j: e.tensor_tensor(out=ys[:, j, :], in0=ys[:, j, :], in1=zs[:, j, :],
                                                           op=ALU.mult),
                     reads=["ys%d" % j, "zs%d" % j], writes=["ys%d" % j])
            for j in range(8):
                if j % 4 == 0:
                    pu = open_piece(("win", 6 + j // 4))
                jq = j % 4
                b = pe_job(mm_cm(pu[0], j % 4, par), reads=[pu[1]] + hn)
                P.op("act", lambda e, b=b, jq=jq: e.activation(out=sg[jq][:, :], in_=banks[b][:, :], func=AF.Gelu),
                     reads=["ps%d" % b], writes=["sg%d" % jq])
                P.op("dve", lambda e, j=j, jq=jq: e.tensor_tensor(out=us[:, j, :], in0=us[:, j, :], in1=sg[jq][:, :],
                                                                  op=ALU.mult),
                     reads=["us%d" % j, "sg%d" % jq], writes=["us%d" % j])
                if j % 4 == 3:
                    close_piece(pu[2])
                if i + 1 < NT and j % 2 == 1:
                    xprep_a(i + 1, j // 2)
                    if j >= 3:
                        xprep_b(i + 1, j // 2 - 1)
            if i + 1 < NT:
                xprep_b(i + 1, 3)
            if i == 0:
                dump("ya", ys, ["ys%d" % k for k in range(8)]); dump("uz", us, ["us%d" % k for k in range(8)])
                dump("meanB", meanB, ["meanB"]); dump("rstdB", rstdB, ["rstdB"]); dump("zs", zs, ["zs%d" % k for k in range(8)])
            pvl = open_piece(("win", 8))
            pvh = open_piece(("win", 9))
            for u in range(4):
                vb = v32[u % 2]
                vname = "big%d" % (8 + u % 2)
                for half, pc in ((0, pvl), (1, pvh)):
                    def vjob(pe, bank, u=u, wbuf=pc[0], par=par):
                        last = None
                        for kc in range(8):
                            last = pe.matmul(bank[:, :], hT[par][:, kc, u * 128:(u + 1) * 128], wbuf[:, kc, :],
                                             start=(kc == 0), stop=(kc == 7))
                        return last
                    b = pe_job(vjob, reads=[pc[1]] + hn)
                    P.op("act", lambda e, b=b, vb=vb, half=half: e.activation(
                        out=vb[:, half * 512:(half + 1) * 512], in_=banks[b][:, :], func=AF.Gelu),
                        reads=["ps%d" % b], writes=[vname])
                k = 2 + u % 2
                small_stats(k, vb, vname, False)
                P.op("dve", lambda e, u=u, vb=vb, k=k: e.tensor_scalar(
                    out=vnb[:, u, :], in0=vb[:, :], scalar1=mv[k][:, 0:1], scalar2=rstd1[k][:, 0:1],
                    op0=ALU.subtract, op1=ALU.mult),
                    reads=[vname, "mv_%d" % k, "rstd1_%d" % k], writes=["vn%d" % u])
            close_piece(pvl[2])
            close_piece(pvh[2])
            if i == 0:
                dump("vn", vnb, ["vn%d" % k for k in range(4)])
            if i + 1 < NT:
                xprep_transposes(i + 1)
            yn_all = ["ys%d" % k for k in range(8)]
            vn_all = ["vn%d" % k for k in range(4)]
            for j in range(8):
                if j % 4 == 0:
                    pga = open_piece(("win", 12 + j // 4))
                    pco = open_piece(("wco", j // 4))
                jb = j % 2
                lc = j % 4
                b = pe_job(mm_cm(pga[0], lc, par), reads=[pga[1]] + hn)
                P.op("act", lambda e, b=b, jb=jb: e.activation(out=sg[jb][:, :], in_=banks[b][:, :], func=AF.Sigmoid),
                     reads=["ps%d" % b], writes=["sg%d" % jb])

                def yajob(pe, bank, lc=lc, wbuf=pco[0]):
                    last = None
                    for kc in range(8):
                        last = pe.matmul(bank[:, :], wbuf[:, kc, lc * 128:(lc + 1) * 128], ys[:, kc, :],
                                         start=(kc == 0), stop=(kc == 7))
                    return last
                b2 = pe_job(yajob, reads=[pco[1]] + yn_all)
                P.op("dve", lambda e, b2=b2, j=j, jb=jb: e.tensor_tensor(out=ms[:, j, :], in0=banks[b2][:, :],
                                                                        in1=sg[jb][:, :], op=ALU.mult),
                     reads=["ps%d" % b2, "sg%d" % jb], writes=["ms%d" % j])
                if j % 4 == 3:
                    close_piece(pga[2])
                    close_piece(pco[2])
                h = j

                def sjob2(pe, bank, h=h):
                    last = None
                    for u in range(4):
                        last = pe.matmul(bank[:, u * 128:(u + 1) * 128], vnb[:, u, h * 128:(h + 1) * 128],
                                         wsTm[:, h, :], start=True, stop=True)
                    return last
                b = pe_job(sjob2, reads=vn_all + ["wsTm"])
                hb = h % 2
                P.op("dve", lambda e, b=b, h=h, hb=hb: e.scalar_tensor_tensor(
                    out=t32a[hb].ap().rearrange("p (u t) -> p u t", u=4),
                    in0=banks[b].ap().rearrange("p (u t) -> p u t", u=4),
                    scalar=col(C_SLG, h),
                    in1=Cb[:, h, :].unsqueeze(1).to_broadcast([128, 4, 128]),
                    op0=ALU.mult, op1=ALU.add),
                    reads=["ps%d" % b, "cols", "Cb"], writes=["t32a%d" % hb])
                P.op("dve", lambda e, h=h, hb=hb: e.tensor_tensor(out=us[:, h, :], in0=t32a[hb][:, :], in1=us[:, h, :],
                                                                  op=ALU.mult),
                     reads=["t32a%d" % hb, "us%d" % h], writes=["us%d" % h])
            if i == 0:
                dump("mA", ms, ["ms%d" % k for k in range(8)])
                dump("yb", us, ["us%d" % k for k in range(8)])
            def xf_load(u):
                row0 = i * T + u * 128
                k = 2 + u % 2
                P.dma("sp", xf[u % 2].ap(), x_d[row0:row0 + 128, :], writes=["big%d" % k], key="ld_big%d" % k)
            xf_load(0)
            xf_load(1)
            un_all = ["us%d" % k for k in range(8)]
            for j in range(8):
                if j % 4 == 0:
                    pgb = open_piece(("win", 14 + j // 4))
                    pso = open_piece(("wso", j // 4))
                jb = j % 2
                lc = j % 4
                b = pe_job(mm_cm(pgb[0], lc, par), reads=[pgb[1]] + hn)
                P.op("act", lambda e, b=b, jb=jb: e.activation(out=sg[jb][:, :], in_=banks[b][:, :], func=AF.Sigmoid),
                     reads=["ps%d" % b], writes=["sg%d" % jb])

                def ybjob(pe, bank, lc=lc, wbuf=pso[0]):
                    last = None
                    for kc in range(8):
                        last = pe.matmul(bank[:, :], wbuf[:, kc, lc * 128:(lc + 1) * 128], us[:, kc, :],
                                         start=(kc == 0), stop=(kc == 7))
                    return last
                b2 = pe_job(ybjob, reads=[pso[1]] + un_all)
                P.op("dve", lambda e, b2=b2, jb=jb: e.tensor_tensor(out=mbt[jb][:, :], in0=banks[b2][:, :],
                                                                   in1=sg[jb][:, :], op=ALU.mult),
                     reads=["ps%d" % b2, "sg%d" % jb], writes=["mbt%d" % jb])
                P.op("dve", lambda e, j=j, jb=jb: e.tensor_tensor(out=ms[:, j, :], in0=ms[:, j, :], in1=mbt[jb][:, :],
                                                                  op=ALU.add),
                     reads=["ms%d" % j, "mbt%d" % jb], writes=["ms%d" % j])
                if j % 4 == 3:
                    close_piece(pgb[2])
                    close_piece(pso[2])
            if i == 0:
                dump("merged", ms, ["ms%d" % k for k in range(8)])
            mn_all = ["ms%d" % k for k in range(8)]
            pol = open_piece(("wo", 0))
            poh = open_piece(("wo", 1))
            for u in range(4):
                xb = xf[u % 2]
                xname = "big%d" % (2 + u % 2)
                for half, pc in ((0, pol), (1, poh)):
                    def ojob(pe, bank, u=u, wbuf=pc[0]):
                        last = None
                        for kc in range(8):
                            last = pe.matmul(bank[:, :], ms[:, kc, u * 128:(u + 1) * 128], wbuf[:, kc, :],
                                             start=(kc == 0), stop=(kc == 7))
                        return last
                    b = pe_job(ojob, reads=[pc[1]] + mn_all)
                    P.op("dve", lambda e, b=b, half=half: e.tensor_tensor(
                        out=t32b[half][:, :], in0=banks[b][:, :], in1=gateb[:, half * 512:(half + 1) * 512],
                        op=ALU.mult),
                        reads=["ps%d" % b, "gateb"], writes=["t32b%d" % half])
                    P.op("dve", lambda e, xb=xb, half=half: e.tensor_tensor(
                        out=xb[:, half * 512:(half + 1) * 512], in0=t32b[half][:, :],
                        in1=xb[:, half * 512:(half + 1) * 512], op=ALU.add),
                        reads=["t32b%d" % half, xname], writes=[xname])

                def tail(u=u, xb=xb, xname=xname, i=i):
                    k = 4 + u % 2
                    small_stats(k, xb, xname, True)
                    for half in range(2):
                        P.op("dve", lambda e, xb=xb, half=half, k=k: e.scalar_tensor_tensor(
                            out=xb[:, half * 512:(half + 1) * 512], in0=xb[:, half * 512:(half + 1) * 512],
                            scalar=rstd1[k][:, 0:1], in1=gfin[:, half * 512:(half + 1) * 512],
                            op0=ALU.mult, op1=ALU.mult),
                            reads=[xname, "rstd1_%d" % k, "gfin"], writes=[xname])
                    row0 = i * T + u * 128
                    P.dma("sp", out_d[row0:row0 + 128, :], xb.ap(), reads=[xname], writes=["out_%d_%d" % (i, u)],
                          key="st_big%d" % (2 + u % 2))
                if u < 2:
                    tail()
                    xf_load(u + 2)
                else:
                    deferred.append(tail)
            close_piece(pol[2])
            close_piece(poh[2])
        for fn in deferred:
            fn()
        del deferred[:]

        P.wait_all("sp")
        P.emit()
    return nc


_NC_CACHE = {}


def _host_layout(inputs, b):
    f = lambda a: np.ascontiguousarray(a, dtype=np.float32)
    colT = lambda v: np.asarray(v, dtype=np.float32).reshape(8, 128).T
    cols = np.stack([colT(inputs["c"][b]), colT(inputs["g_pre"][0]), colT(inputs["conv_b"][0]),
                     colT(inputs["conv_ln_g"][0]), colT(inputs["conv_ln_b"][0]),
                     colT(inputs["sgu_ln_g"][0]), colT(inputs["sgu_ln_b"][0])], axis=1).reshape(128, 56)
    return {"x": f(inputs["x"][b]), "cols": f(cols)}


def kernel(**inputs):
    inputs = {k: np.asarray(v) for k, v in inputs.items()}
    n = 8
    f = lambda a: np.ascontiguousarray(a, dtype=np.float32)
    cw = np.concatenate([inputs["conv_w"][0], np.zeros((1, D), np.float32)], axis=0)
    q = np.arange(8)[:, None]
    s = np.arange(4)[None, :]
    kidx = 30 - 4 * q - s
    kidx = np.where(kidx < 0, 31, kidx)
    g5 = cw[kidx]
    g5 = g5.reshape(8, 4, 8, 4, 32)
    cwp = np.transpose(g5, (1, 4, 2, 3, 0)).reshape(128, 256)
    wsT = np.transpose(inputs["w_sgu"][0], (2, 0, 1)).reshape(128, 1024)
    bs_bc = np.broadcast_to(inputs["b_sgu"][0].reshape(1, 1024), (128, 1024))
    bada_bc = np.broadcast_to(inputs["b_ada"][0].reshape(1, 3 * D), (128, 3 * D))
    gfin_bc = np.broadcast_to(inputs["g_final"].reshape(1, D), (128, D))
    ident = np.eye(128, dtype=np.float32)
    maskT = (np.arange(128)[:, None] <= np.arange(128)[None, :]).astype(np.float32)
    E = np.tile(np.eye(32, dtype=np.float32), (4, 1))
    consts = np.concatenate([ident, maskT, E], axis=1)
    shared = {
        "w_ada": f(inputs["w_ada"][0]), "b_ada_bc": f(bada_bc), "w_in": f(inputs["w_in"][0]),
        "cwp": f(cwp), "w_conv_out": f(inputs["w_conv_out"][0]), "w_sgu_out": f(inputs["w_sgu_out"][0]),
        "w_o": f(inputs["w_o"][0]), "wsT": f(wsT), "bs_bc": f(bs_bc), "gfin_bc": f(gfin_bc),
        "consts": f(consts),
    }
    in_maps = []
    for b in range(n):
        m = dict(shared)
        m.update(_host_layout(inputs, b))
        in_maps.append(m)
    if "nc" not in _NC_CACHE:
        _NC_CACHE["nc"] = build_nc()
    nc = _NC_CACHE["nc"]
    res = run_bass_kernel_spmd(nc, in_maps, core_ids=list(range(n)))
    out = np.stack([np.asarray(res.results[b]["out"], dtype=np.float32) for b in range(n)], axis=0)
    return out
```

```python
import numpy as np
import concourse.bass as bass
import concourse.mybir as mybir
from concourse.bass_utils import run_bass_kernel_spmd
from contextlib import ExitStack

F32 = mybir.dt.float32
BF16 = mybir.dt.bfloat16
ALU = mybir.AluOpType
AF = mybir.ActivationFunctionType

ENG = ("pe", "act", "dve", "pool", "sp")

D = 1024
S = 4096
T = 512
NT = S // T
EPS = 1e-6
AH = 32
RW = 540
RH = 28


class Prog:
    def __init__(self, nc, stack):
        self.nc = nc
        self.items = {e: [] for e in ENG}
        self.esem = {e: stack.enter_context(nc.semaphore("s_" + e)) for e in ENG if e != "sp"}
        self.ecnt = {e: 0 for e in ENG}
        self.seen = {e: {} for e in ENG}
        self.dsem = {}
        self.stack = stack
        self.buf = {}

    def _st(self, b):
        st = self.buf.get(b)
        if st is None:
            st = {"w": None, "r": {}}
            self.buf[b] = st
        return st

    def _deps(self, e, reads, writes):
        deps = []
        for b in reads:
            st = self._st(b)
            if st["w"] is not None:
                deps.append((st["w"], True))
        for b in writes:
            st = self._st(b)
            if st["w"] is not None:
                deps.append((st["w"], False))
            for t in st["r"].values():
                deps.append((t, False))
        need = {}
        for (sem, val, src), raw in deps:
            if src == e:
                if e == "pe":
                    continue
                if e in ("act", "dve", "pool") and not raw:
                    continue
            k = id(sem)
            if val > need.get(k, (None, 0))[1]:
                need[k] = (sem, val)
        for k, (sem, val) in need.items():
            if self.seen[e].get(k, 0) >= val:
                continue
            self.seen[e][k] = val
            self.items[e].append(("wait", sem, val))

    def _mark(self, tok, reads, writes):
        for b in reads:
            st = self._st(b)
            k = id(tok[0])
            old = st["r"].get(k)
            if old is None or old[1] < tok[1]:
                st["r"][k] = tok
        for b in writes:
            st = self._st(b)
            st["w"] = tok
            st["r"] = {}

    def op(self, e, fn, reads=(), writes=()):
        self._deps(e, reads, writes)
        self.ecnt[e] += 1
        tok = (self.esem[e], self.ecnt[e], e)
        self.items[e].append(("op", fn, self.esem[e], 1))
        self._mark(tok, reads, writes)
        return tok

    def dma(self, q, out, in_, reads=(), writes=(), key=None, **kw):
        self._deps(q, reads, writes)
        if key not in self.dsem:
            self.dsem[key] = [self.stack.enter_context(self.nc.semaphore("d_" + str(key))), 0]
        ent = self.dsem[key]
        ent[1] += 16
        tok = (ent[0], ent[1], "dma:" + str(key))

        def fn(eng, out=out, in_=in_, kw=kw):
            return eng.dma_start(out=out, in_=in_, **kw)

        self.items[q].append(("op", fn, ent[0], 16))
        self._mark(tok, reads, writes)
        return tok

    def retag(self, names, tok):
        for b in names:
            self._st(b)["w"] = tok

    def wait_all(self, e):
        need = {}
        for st in self.buf.values():
            toks = list(st["r"].values())
            if st["w"] is not None:
                toks.append(st["w"])
            for sem, val, src in toks:
                k = id(sem)
                if val > need.get(k, (None, 0))[1]:
                    need[k] = (sem, val)
        for k, (sem, val) in need.items():
            if self.seen[e].get(k, 0) >= val:
                continue
            self.seen[e][k] = val
            self.items[e].append(("wait", sem, val))

    def emit(self):
        nc = self.nc
        items = self.items

        def replay(name, eng):
            for it in items[name]:
                if it[0] == "wait":
                    eng.wait_ge(it[1], it[2])
                else:
                    ins = it[1](eng)
                    ins.then_inc(it[2], it[3])

        with nc.Block() as block:
            @block.tensor
            def _(pe):
                replay("pe", pe)

            @block.scalar
            def _(act):
                replay("act", act)

            @block.vector
            def _(dve):
                replay("dve", dve)

            @block.gpsimd
            def _(pool):
                replay("pool", pool)

            @block.sync
            def _(sp):
                replay("sp", sp)


C_C, C_GPRE, C_CONVB, C_CLG, C_CLB, C_SLG, C_SLB = range(7)


def build_nc(NT=NT, debug=False):
    nc = bass.Bass("TRN2", target_bir_lowering=False)
    dbg_n = [0]
    dt_in = lambda name, shape: nc.dram_tensor(name, shape, F32, kind="ExternalInput").ap()
    x_d = dt_in("x", [S, D])
    wada_d = dt_in("w_ada", [D, 3 * D])
    bada_d = dt_in("b_ada_bc", [128, 3 * D])
    win_d = dt_in("w_in", [D, 8 * D])
    cwp_d = dt_in("cwp", [128, 256])
    cols_d = dt_in("cols", [128, 56])
    wco_d = dt_in("w_conv_out", [D, D])
    wso_d = dt_in("w_sgu_out", [D, D])
    wo_d = dt_in("w_o", [D, D])
    wsT_d = dt_in("wsT", [128, 1024])
    bs_d = dt_in("bs_bc", [128, 1024])
    gfin_d = dt_in("gfin_bc", [128, 1024])
    cst_d = dt_in("consts", [128, 288])
    out_d = nc.dram_tensor("out", [S, D], F32, kind="ExternalOutput").ap()
    scr_win = nc.dram_tensor("scr_win", [D, 8 * D], BF16).ap()
    scr_wco = nc.dram_tensor("scr_wco", [D, D], BF16).ap()
    scr_wso = nc.dram_tensor("scr_wso", [D, D], BF16).ap()
    scr_wo = nc.dram_tensor("scr_wo", [D, D], BF16).ap()

    with ExitStack() as stk:
        P = Prog(nc, stk)
        sb = lambda name, shape, dtype: nc.alloc_sbuf_tensor("sb_" + name, shape, dtype)
        cst = sb("cst", [128, 288], F32)
        ident32 = cst[:, 0:128]
        maskT = cst[:, 128:256]
        Emask = cst[:, 256:288]
        ones_bf = sb("ones_bf", [128, 128], BF16)
        ones32 = sb("ones32", [128, 128], F32)
        neghalf = sb("neghalf", [128, 512], F32)
        cols = sb("cols", [128, 56], F32)
        a_col = sb("a_col", [128, 8], F32)
        shift_col = sb("shift_col", [128, 8], F32)
        cwp = sb("cwp", [128, 256], F32)
        Wk = sb("Wk", [128, 8, 4, 8, 32], BF16)
        wsTm = sb("wsTm", [128, 8, 128], BF16)
        Cb = sb("Cb", [128, 8, 128], F32)
        gfin = sb("gfin", [128, 1024], F32)
        big = [None, None] + [sb("big%d" % i, [128, 1024], F32) for i in range(2, 10)]
        vnb = sb("vn", [128, 4, 1024], BF16)
        xf = big[2:4]
        xn = big[4:8]
        v32 = big[8:10]
        hT = [sb("hT%d" % i, [128, 8, T], BF16) for i in range(2)]
        aT = sb("aT", [128, 8, AH + T], BF16)
        Rb = [sb("R%d" % i, [128, 2, 4, RW], BF16) for i in range(2)]
        ys = sb("ys", [128, 8, T], BF16)
        zs = sb("zs", [128, 8, T], BF16)
        us = sb("us", [128, 8, T], BF16)
        ms = sb("ms", [128, 8, T], BF16)
        sg = [sb("sg%d" % i, [128, T], BF16) for i in range(4)]
        ysq = [sb("ysq%d" % i, [128, T], BF16) for i in range(2)]
        mbt = [sb("mbt%d" % i, [128, T], BF16) for i in range(2)]
        t32a = [sb("t32a%d" % i, [128, T], F32) for i in range(2)]
        t32b = [sb("t32b%d" % i, [128, T], F32) for i in range(2)]
        meanB = sb("meanB", [128, T], F32)
        mbt32 = sb("mbt32", [128, T], F32)
        msqB = sb("msqB", [128, T], F32)
        rstdB = sb("rstdB", [128, T], F32)
        wb = [sb("wb%d" % i, [128, 8, T], BF16) for i in range(4)]
        gateb = sb("gateb", [128, 1024], F32)
        NSM = 10
        st6 = [sb("st6_%d" % i, [128, 12], F32) for i in range(NSM)]
        mv = [sb("mv_%d" % i, [128, 2], F32) for i in range(NSM)]
        tmp1 = [sb("tmp1_%d" % i, [128, 1], F32) for i in range(NSM)]
        rstd1 = [sb("rstd1_%d" % i, [128, 1], F32) for i in range(NSM)]
        banks = [nc.alloc_psum_tensor("ps%d" % i, [128, 512], F32) for i in range(8)]
        NRING = 6
        ring = [0]

        def dump(tag, t, names):
            if not debug:
                return
            shp = list(t.shape)
            flat = [shp[0], int(np.prod(shp[1:]))]
            d = nc.dram_tensor("dbg_" + tag, flat, t.dtype, kind="ExternalOutput").ap()
            a = t.ap()
            if len(shp) == 3:
                a = a.rearrange("p a b -> p (a b)")
            elif len(shp) == 5:
                a = a.rearrange("p a b c d -> p (a b c d)")
            P.dma("sp", d, a, reads=names, writes=["dbgout_" + tag], key="dbg")

        def col(k, j):
            return cols[:, k * 8 + j:k * 8 + j + 1]

        def nextbank():
            b = ring[0] % NRING
            ring[0] += 1
            return b

        def pe_job(fn, reads, b=None):
            if b is None:
                b = nextbank()
            P.op("pe", lambda pe, fn=fn, b=b: fn(pe, banks[b]), reads=reads, writes=["ps%d" % b])
            return b

        P.dma("sp", cst.ap(), cst_d, writes=["cst"], key="c0")
        P.dma("sp", cols.ap(), cols_d, writes=["cols"], key="c0")
        P.dma("sp", cwp.ap(), cwp_d, writes=["cwp"], key="c0")
        tokc = P.dma("sp", gfin.ap(), gfin_d, writes=["gfin"], key="c0")
        P.retag(["cst", "cols", "cwp", "gfin"], tokc)
        P.op("dve", lambda e: e.memset(ones_bf.ap(), 1.0), writes=["ones_bf"])
        P.op("dve", lambda e: e.memset(ones32.ap(), 1.0), writes=["ones32"])
        P.op("dve", lambda e: e.memset(neghalf.ap(), -0.5), writes=["neghalf"])
        P.op("dve", lambda e: e.memset(aT.ap(), 0.0), writes=["aT%d" % j for j in range(8)] + ["aTh%d" % j for j in range(8)])

        piece_src = {}
        cast_done = set()
        for cb in range(16):
            piece_src[("win", cb)] = (scr_win, cb * T, ["scr_win%d" % cb])
        for h in range(2):
            piece_src[("wco", h)] = (scr_wco, h * T, ["scr_wco"])
            piece_src[("wso", h)] = (scr_wso, h * T, ["scr_wso"])
            piece_src[("wo", h)] = (scr_wo, h * T, ["scr_wo"])

        def ensure_cast(kind):
            k0 = kind[0]
            tag = kind if k0 == "win" else (k0,)
            if tag in cast_done:
                return
            cast_done.add(tag)
            if k0 == "win":
                cb = kind[1]
                P.dma("pool", scr_win[:, cb * T:(cb + 1) * T], win_d[:, cb * T:(cb + 1) * T],
                      reads=([] if cb in (2, 0) else ["a_col"]),
                      writes=["scr_win%d" % cb], key="cast%d" % (len(cast_done) % 6))
            elif k0 == "wco":
                P.dma("pool", scr_wco, wco_d, reads=["a_col"], writes=["scr_wco"], key="cast%d" % (len(cast_done) % 6))
            elif k0 == "wso":
                P.dma("pool", scr_wso, wso_d, reads=["a_col"], writes=["scr_wso"], key="cast%d" % (len(cast_done) % 6))
            else:
                P.dma("pool", scr_wo, wo_d, reads=["a_col"], writes=["scr_wo"], key="cast%d" % (len(cast_done) % 6))

        for kind in (("win", 2), ("win", 0)):
            ensure_cast(kind)

        tile_pieces = [("win", 2), ("win", 0), ("win", 3), ("win", 1), ("win", 4), ("win", 5),
                       ("win", 10), ("win", 11), ("win", 6), ("win", 7),
                       ("win", 8), ("win", 9),
                       ("win", 12), ("wco", 0), ("win", 13), ("wco", 1),
                       ("win", 14), ("wso", 0), ("win", 15), ("wso", 1), ("wo", 0), ("wo", 1)]
        all_pieces = []
        for i in range(NT):
            all_pieces += tile_pieces
        issued = [0]
        opened = [0]

        def issue_upto(m):
            while issued[0] < min(m, len(all_pieces)):
                n = issued[0]
                for la in range(n, min(n + 5, len(all_pieces))):
                    ensure_cast(all_pieces[la])
                src, c0, snames = piece_src[all_pieces[n]]
                k = n % 4
                P.dma("sp", wb[k].ap(), src.rearrange("(kc p) n -> p kc n", p=128)[:, :, c0:c0 + T],
                      reads=snames, writes=["wb%d" % k], key="wb%d" % k)
                issued[0] += 1

        def open_piece(kind):
            n = opened[0]
            assert all_pieces[n] == kind, (all_pieces[n], kind)
            opened[0] += 1
            issue_upto(n + 1)
            return wb[n % 4], "wb%d" % (n % 4), n

        def close_piece(n):
            issue_upto(n + 4 + 1)

        def small_stats(k, src, sname, rms):
            P.op("dve", lambda e: e.bn_stats(out=st6[k][:, 0:6], in_=src[:, 0:512]), reads=[sname], writes=["st6a_%d" % k])
            P.op("dve", lambda e: e.bn_stats(out=st6[k][:, 6:12], in_=src[:, 512:1024]), reads=[sname], writes=["st6b_%d" % k])
            P.op("dve", lambda e: e.bn_aggr(out=mv[k][:, :], in_=st6[k][:, :]),
                 reads=["st6a_%d" % k, "st6b_%d" % k], writes=["mv_%d" % k])
            if rms:
                P.op("dve", lambda e: e.scalar_tensor_tensor(
                    out=tmp1[k][:, :], in0=mv[k][:, 0:1], scalar=mv[k][:, 0:1], in1=mv[k][:, 1:2],
                    op0=ALU.mult, op1=ALU.add), reads=["mv_%d" % k], writes=["tmp1_%d" % k])
                P.op("dve", lambda e: e.tensor_scalar(out=tmp1[k][:, :], in0=tmp1[k][:, :], scalar1=EPS, scalar2=None,
                                                      op0=ALU.add), reads=["tmp1_%d" % k], writes=["tmp1_%d" % k])
            else:
                P.op("dve", lambda e: e.tensor_scalar(out=tmp1[k][:, :], in0=mv[k][:, 1:2], scalar1=EPS, scalar2=None,
                                                      op0=ALU.add), reads=["mv_%d" % k], writes=["tmp1_%d" % k])
            P.op("pool", lambda e: e.tensor_tensor(out=rstd1[k][:, :], in0=tmp1[k][:, :], in1=neghalf[:, 0:1],
                                                   op=ALU.pow),
                 reads=["tmp1_%d" % k, "neghalf"], writes=["rstd1_%d" % k])

        def xprep_a(i, u):
            row0 = i * T + u * 128
            nm = "big%d" % (4 + u)
            P.dma("sp", xn[u].ap(), x_d[row0:row0 + 128, :], writes=[nm], key="ld_" + nm)
            small_stats(6 + u, xn[u], nm, True)

        def xprep_b(i, u):
            k = 6 + u
            nm = "big%d" % (4 + u)
            P.op("dve", lambda e: e.tensor_scalar(out=xn[u][:, :], in0=xn[u][:, :], scalar1=rstd1[k][:, 0:1],
                                                  scalar2=None, op0=ALU.mult),
                 reads=[nm, "rstd1_%d" % k], writes=[nm])

        def xprep_transposes(i):
            par = i % 2
            for kc in range(8):
                def tjob(pe, bank, kc=kc):
                    last = None
                    for u in range(4):
                        last = pe.transpose(out=bank[:, u * 128:(u + 1) * 128],
                                            in_=xn[u][:, kc * 128:(kc + 1) * 128], identity=ident32)
                    return last
                b = pe_job(tjob, reads=["big4", "big5", "big6", "big7", "cst"])
                P.op("act", lambda e, b=b, kc=kc: e.activation(
                    out=hT[par][:, kc, :], in_=banks[b][:, :], func=AF.Identity,
                    bias=shift_col[:, kc:kc + 1], scale=a_col[:, kc:kc + 1]),
                    reads=["ps%d" % b, "shift_col", "a_col"], writes=["hT%d_%d" % (par, kc)])

        def hT_names(par):
            return ["hT%d_%d" % (par, kc) for kc in range(8)]

        def mm_cm(wbuf, lc, par):
            def fn(pe, bank):
                last = None
                for kc in range(8):
                    last = pe.matmul(bank[:, :], wbuf[:, kc, lc * 128:(lc + 1) * 128], hT[par][:, kc, :],
                                     start=(kc == 0), stop=(kc == 7))
                return last
            return fn

        for u in range(4):
            xprep_a(0, u)

        t512 = [(t32a[0], "t32a0"), (t32a[1], "t32a1"), (t32b[0], "t32b0"), (t32b[1], "t32b1")]
        tb = [0]

        def next_t512():
            r = t512[tb[0] % len(t512)]
            tb[0] += 1
            return r

        t512b = [(meanB, "meanB"), (msqB, "msqB"), (rstdB, "rstdB"), (mbt32, "mbt32")]
        tb2 = [0]

        def next_t512b():
            r = t512b[tb2[0] % len(t512b)]
            tb2[0] += 1
            return r

        def ada_block(acc, aname, cb, queue="sp", split=False):
            for kc in range(8):
                for half in range(2):
                    sl = slice(half * 512, (half + 1) * 512)
                    buf, bname = next_t512b() if split else next_t512()
                    c0 = cb * 1024 + half * 512
                    P.dma(queue, buf.ap(), wada_d[kc * 128:(kc + 1) * 128, c0:c0 + 512],
                          writes=[bname], key="ld_" + bname)
                    if split:
                        if kc == 0:
                            P.op("act", lambda e, buf=buf, sl=sl: e.activation(
                                out=acc[:, sl], in_=buf[:, :], func=AF.Identity, scale=col(C_C, 0)),
                                reads=[bname, "cols"], writes=[aname])
                        else:
                            P.op("act", lambda e, buf=buf, kc=kc: e.activation(
                                out=buf[:, :], in_=buf[:, :], func=AF.Identity, scale=col(C_C, kc)),
                                reads=[bname, "cols"], writes=[bname])
                            P.op("pool", lambda e, buf=buf, sl=sl: e.tensor_tensor(
                                out=acc[:, sl], in0=acc[:, sl], in1=buf[:, :], op=ALU.add),
                                reads=[bname, aname], writes=[aname])
                    elif kc == 0:
                        P.op("dve", lambda e, buf=buf, sl=sl: e.tensor_scalar(
                            out=acc[:, sl], in0=buf[:, :], scalar1=col(C_C, 0), scalar2=None, op0=ALU.mult),
                            reads=[bname, "cols"], writes=[aname])
                    else:
                        P.op("dve", lambda e, buf=buf, sl=sl, kc=kc: e.scalar_tensor_tensor(
                            out=acc[:, sl], in0=buf[:, :], scalar=col(C_C, kc), in1=acc[:, sl],
                            op0=ALU.mult, op1=ALU.add),
                            reads=[bname, "cols", aname], writes=[aname])
            for half in range(2):
                sl = slice(half * 512, (half + 1) * 512)
                buf, bname = next_t512()
                c0 = cb * 1024 + half * 512
                P.dma(queue, buf.ap(), bada_d[:, c0:c0 + 512], writes=[bname], key="ld_" + bname)
                b = pe_job(lambda pe, bank, sl=sl: pe.matmul(bank[:, :], ones32[:, :], acc[:, sl],
                                                             start=True, stop=True),
                           reads=["ones32", aname])
                P.op("dve", lambda e, b=b, sl=sl, buf=buf: e.tensor_tensor(
                    out=acc[:, sl], in0=banks[b][:, :], in1=buf[:, :], op=ALU.add),
                    reads=["ps%d" % b, bname], writes=[aname])

        accs = [(v32[0], "big8"), (v32[1], "big9")]
        for kc in range(8):
            for cb in range(2):
                acc, aname = accs[cb]
                for half in range(2):
                    sl = slice(half * 512, (half + 1) * 512)
                    buf, bname = next_t512()
                    c0 = cb * 1024 + half * 512
                    P.dma("sp", buf.ap(), wada_d[kc * 128:(kc + 1) * 128, c0:c0 + 512], writes=[bname], key="ld_" + bname)
                    if kc == 0:
                        P.op("dve", lambda e, buf=buf, sl=sl, acc=acc: e.tensor_scalar(
                            out=acc[:, sl], in0=buf[:, :], scalar1=col(C_C, 0), scalar2=None, op0=ALU.mult),
                            reads=[bname, "cols"], writes=[aname])
                    else:
                        P.op("dve", lambda e, buf=buf, sl=sl, kc=kc, acc=acc: e.scalar_tensor_tensor(
                            out=acc[:, sl], in0=buf[:, :], scalar=col(C_C, kc), in1=acc[:, sl],
                            op0=ALU.mult, op1=ALU.add),
                            reads=[bname, "cols", aname], writes=[aname])
        for cb in range(2):
            acc, aname = accs[cb]
            for half in range(2):
                sl = slice(half * 512, (half + 1) * 512)
                buf, bname = next_t512()
                c0 = cb * 1024 + half * 512
                P.dma("sp", buf.ap(), bada_d[:, c0:c0 + 512], writes=[bname], key="ld_" + bname)
                b = pe_job(lambda pe, bank, sl=sl, acc=acc: pe.matmul(bank[:, :], ones32[:, :], acc[:, sl],
                                                                      start=True, stop=True),
                           reads=["ones32", aname])
                P.op("dve", lambda e, b=b, sl=sl, buf=buf, acc=acc: e.tensor_tensor(
                    out=acc[:, sl], in0=banks[b][:, :], in1=buf[:, :], op=ALU.add),
                    reads=["ps%d" % b, bname], writes=[aname])
        gslots = []
        for slot, nm, per in ((zs, "zs", 2), (us, "us", 2), (ms, "ms", 2)):
            v = slot.ap().rearrange("p a b -> p (a b)").bitcast(F32)
            for q in range(4):
                gslots.append((v[:, q * 512:(q + 1) * 512], ["%s%d" % (nm, 2 * q), "%s%d" % (nm, 2 * q + 1)]))
        v = vnb.ap().rearrange("p a b -> p (a b)").bitcast(F32)
        for q in range(4):
            gslots.append((v[:, q * 512:(q + 1) * 512], ["vn%d" % q]))
        gi = 0
        gate_ops = []
        issue_upto(2)
        gtok = None
        gall = []
        for kc in range(8):
            for half in range(2):
                sl = slice(half * 512, (half + 1) * 512)
                gap_, gnames = gslots[gi]
                gi += 1
                c0 = 2 * 1024 + half * 512
                gtok = P.dma("sp", gap_, wada_d[kc * 128:(kc + 1) * 128, c0:c0 + 512], writes=gnames, key="ld_gate")
                gate_ops.append((kc, sl, gap_, gnames))
                gall += gnames
        P.retag(gall, gtok)
        P.dma("sp", big[2].ap(), bada_d[:, 2048:3072], writes=["big2"], key="ld_big2")

        def gate_finalize():
            for kc, sl, gap_, gnames in gate_ops:
                if kc == 0:
                    P.op("dve", lambda e, sl=sl, gap_=gap_: e.tensor_scalar(
                        out=gateb[:, sl], in0=gap_, scalar1=col(C_C, 0), scalar2=None, op0=ALU.mult),
                        reads=gnames + ["cols"], writes=["gateb"])
                else:
                    P.op("dve", lambda e, sl=sl, gap_=gap_, kc=kc: e.scalar_tensor_tensor(
                        out=gateb[:, sl], in0=gap_, scalar=col(C_C, kc), in1=gateb[:, sl],
                        op0=ALU.mult, op1=ALU.add),
                        reads=gnames + ["cols", "gateb"], writes=["gateb"])
            for half in range(2):
                sl = slice(half * 512, (half + 1) * 512)
                b = pe_job(lambda pe, bank, sl=sl: pe.matmul(bank[:, :], ones32[:, :], gateb[:, sl],
                                                             start=True, stop=True),
                           reads=["ones32", "gateb"])
                P.op("dve", lambda e, b=b, sl=sl: e.tensor_tensor(
                    out=gateb[:, sl], in0=banks[b][:, :], in1=big[2][:, sl], op=ALU.add),
                    reads=["ps%d" % b, "big2"], writes=["gateb"])

        for u in range(4):
            xprep_b(0, u)
        cwp5 = cwp.ap().rearrange("p (j g q) -> p j g q", j=8, g=4)
        for j in range(8):
            for g in range(4):
                P.op("pool", lambda e, j=j, g=g: e.tensor_tensor(
                    out=Wk[:, j, g, :, :],
                    in0=Emask.unsqueeze(1).to_broadcast([128, 8, 32]),
                    in1=cwp5[:, j, g, :].unsqueeze(2).to_broadcast([128, 8, 32]),
                    op=ALU.mult), reads=["cst", "cwp"], writes=["Wk"])


        def colx(pe, bank):
            last = None
            for idx in range(16):
                src = v32[idx // 8]
                kc = idx % 8
                last = pe.matmul(bank[:, 2 * idx:2 * idx + 2], src[:, kc * 128:(kc + 1) * 128], ident32[:, 0:2],
                                 start=True, stop=True)
            return last
        b = pe_job(colx, reads=["big8", "big9", "cst"])
        P.op("dve", lambda e, b=b: e.tensor_copy(out=shift_col[:, :], in_=banks[b][:, 0:16:2]),
             reads=["ps%d" % b], writes=["shift_col"])
        P.op("dve", lambda e, b=b: e.scalar_tensor_tensor(
            out=a_col[:, :], in0=banks[b][:, 16:32:2], scalar=1.0, in1=cols[:, C_GPRE * 8:C_GPRE * 8 + 8],
            op0=ALU.add, op1=ALU.mult), reads=["ps%d" % b, "cols"], writes=["a_col"])
        dump("a_col", a_col, ["a_col"]); dump("shift_col", shift_col, ["shift_col"])
        issue_upto(4)
        xprep_transposes(0)

        P.dma("sp", big[8].ap(), wsT_d, writes=["big8"], key="ld_big8")
        P.op("dve", lambda e: e.tensor_tensor(
            out=wsTm[:, :, :], in0=big[8].ap().rearrange("p (h t) -> p h t", h=8),
            in1=maskT.unsqueeze(1).to_broadcast([128, 8, 128]), op=ALU.mult),
            reads=["big8", "cst"], writes=["wsTm"])
        P.dma("sp", big[3].ap(), bs_d, writes=["big3"], key="ld_big3")
        wsflat = wsTm.ap().rearrange("p h t -> p (h t)")
        for half in range(2):
            b = pe_job(lambda pe, bank, half=half: pe.matmul(bank[:, :], ones_bf[:, :],
                                                             wsflat[:, half * 512:(half + 1) * 512],
                                                             start=True, stop=True),
                       reads=["ones_bf", "wsTm"])
            for hh in range(4):
                h = half * 4 + hh
                P.op("dve", lambda e, b=b, h=h, hh=hh: e.scalar_tensor_tensor(
                    out=Cb[:, h, :], in0=banks[b][:, hh * 128:(hh + 1) * 128], scalar=col(C_SLB, h),
                    in1=big[3][:, h * 128:(h + 1) * 128], op0=ALU.mult, op1=ALU.add),
                    reads=["ps%d" % b, "cols", "big3"], writes=["Cb"])
        dump("Cb", Cb, ["Cb"]); dump("wsTm", wsTm, ["wsTm"]); dump("Wk", Wk, ["Wk"])

        dump("hT0", hT[0], hT_names(0))
        deferred = [gate_finalize]
        for i in range(NT):
            par = i % 2
            hn = hT_names(par)
            def conv(j):
                pb = (j // 2) % 2
                jj = j % 2
                jb = j % 2
                rn = ["R%d_%d_%d" % (pb, g, s) for g in range(4) for s in range(4)]

                def cjob(pe, bank):
                    last = None
                    for q in range(8):
                        for g in range(4):
                            last = pe.matmul(bank[32 * g:32 * g + 32, :], Wk[:, j, g, q, :],
                                             Rb[pb][:, jj, g, RH - 4 * q:RH - 4 * q + T],
                                             start=(q == 0), stop=(q == 7), tile_position=(0, 32 * g))
                    return last
                b = pe_job(cjob, reads=rn + ["Wk"])
                P.op("act", lambda e, b=b: e.activation(out=ys[:, j, :], in_=banks[b][:, :], func=AF.Identity,
                                                        bias=col(C_CONVB, j), scale=1.0),
                     reads=["ps%d" % b, "cols"], writes=["ys%d" % j])
                P.op("act", lambda e, b=b: e.activation(out=ysq[jb][:, :], in_=banks[b][:, :], func=AF.Square,
                                                        bias=col(C_CONVB, j), scale=1.0),
                     reads=["ps%d" % b, "cols"], writes=["ysq%d" % jb])

            def stats(j):
                jb = j % 2

                def sjob(pe):
                    pe.matmul(banks[6][:, :], ones_bf[:, :], ys[:, j, :], start=(j == 0), stop=(j == 7))
                    return pe.matmul(banks[7][:, :], ones_bf[:, :], ysq[jb][:, :], start=(j == 0), stop=(j == 7))
                P.op("pe", sjob, reads=["ones_bf", "ys%d" % j, "ysq%d" % jb], writes=["ps6", "ps7"])

            def glu_val(j, pg, pv):
                lc = j % 4
                jb = j % 2
                b1 = pe_job(mm_cm(pg[0], lc, par), reads=[pg[1]] + hn)
                P.op("act", lambda e, b1=b1, jb=jb: e.activation(out=sg[jb][:, :], in_=banks[b1][:, :],
                                                                 func=AF.Sigmoid),
                     reads=["ps%d" % b1], writes=["sg%d" % jb])
                b2 = pe_job(mm_cm(pv[0], lc, par), reads=[pv[1]] + hn)
                P.op("dve", lambda e, b2=b2, jb=jb, j=j: e.tensor_tensor(
                    out=aT[:, j, AH:AH + T], in0=banks[b2][:, :], in1=sg[jb][:, :], op=ALU.mult),
                    reads=["ps%d" % b2, "sg%d" % jb], writes=["aT%d" % j])

            def replicas(p):
                pb = p % 2
                j0 = 2 * p
                for q, key, gs in (("sp", "R%d" % pb, (0, 1)), ("pool", "RP%d" % pb, (2, 3))):
                    rn = []
                    tok = None
                    for g in gs:
                        for s in range(4):
                            nm = "R%d_%d_%d" % (pb, g, s)
                            rn.append(nm)
                            c0 = AH - RH - s
                            tok = P.dma(q, Rb[pb][32 * s:32 * s + 32, :, g, :],
                                        aT[32 * g:32 * g + 32, j0:j0 + 2, c0:c0 + RW],
                                        reads=["aT%d" % j0, "aTh%d" % j0, "aT%d" % (j0 + 1), "aTh%d" % (j0 + 1)],
                                        writes=[nm], key=key)
                    P.retag(rn, tok)

            def zA_job(j, pz):
                b = pe_job(mm_cm(pz[0], j % 4, par), reads=[pz[1]] + hn)
                P.op("act", lambda e, b=b, j=j: e.activation(out=zs[:, j, :], in_=banks[b][:, :], func=AF.Silu),
                     reads=["ps%d" % b], writes=["zs%d" % j])

            pg = open_piece(("win", 2))
            pv = open_piece(("win", 0))
            glu_val(0, pg, pv); glu_val(1, pg, pv); replicas(0)
            for fn in deferred:
                fn()
            del deferred[:]
            glu_val(2, pg, pv); glu_val(3, pg, pv); replicas(1)
            close_piece(pg[2]); close_piece(pv[2])
            pg = open_piece(("win", 3))
            pv = open_piece(("win", 1))
            glu_val(4, pg, pv)
            conv(0); conv(1)
            glu_val(5, pg, pv); replicas(2)
            glu_val(6, pg, pv)
            stats(0); stats(1); conv(2); conv(3)
            glu_val(7, pg, pv); replicas(3)
            close_piece(pg[2]); close_piece(pv[2])
            pz = open_piece(("win", 4))
            zA_job(0, pz)
            stats(2); stats(3); conv(4); conv(5)
            zA_job(1, pz); zA_job(2, pz); zA_job(3, pz)
            close_piece(pz[2])
            pz = open_piece(("win", 5))
            zA_job(4, pz)
            stats(4); stats(5); conv(6); conv(7)
            zA_job(5, pz); zA_job(6, pz)
            stats(6); stats(7)
            zA_job(7, pz)
            close_piece(pz[2])
            for j in range(8):
                P.op("pool", lambda e, j=j: e.tensor_copy(out=aT[:, j, 0:AH], in_=aT[:, j, T:T + AH]),
                     reads=["aT%d" % j], writes=["aTh%d" % j])
            if i == 0:
                dump("aT", aT, ["aT%d" % k for k in range(8)] + ["aTh%d" % k for k in range(8)])
                dump("yconv", ys, ["ys%d" % k for k in range(8)])
            P.op("dve", lambda e: e.tensor_scalar(out=meanB[:, :], in0=banks[6][:, :], scalar1=1.0 / D, scalar2=None,
                                                  op0=ALU.mult), reads=["ps6"], writes=["meanB"])
            P.op("dve", lambda e: e.tensor_tensor(out=msqB[:, :], in0=meanB[:, :], in1=meanB[:, :], op=ALU.mult),
                 reads=["meanB"], writes=["msqB"])
            P.op("dve", lambda e: e.scalar_tensor_tensor(out=rstdB[:, :], in0=banks[7][:, :], scalar=1.0 / D,
                                                         in1=msqB[:, :], op0=ALU.mult, op1=ALU.subtract),
                 reads=["ps7", "msqB"], writes=["rstdB"])
            P.op("dve", lambda e: e.tensor_scalar(out=rstdB[:, :], in0=rstdB[:, :], scalar1=0.0, scalar2=EPS,
                                                  op0=ALU.max, op1=ALU.add), reads=["rstdB"], writes=["rstdB"])
            P.op("act", lambda e: e.activation(out=msqB[:, :], in_=rstdB[:, :], func=AF.Sqrt),
                 reads=["rstdB"], writes=["msqB"])
            P.op("dve", lambda e: e.reciprocal(out=rstdB[:, :], in_=msqB[:, :]), reads=["msqB"], writes=["rstdB"])
            for j in range(8):
                if j % 4 == 0:
                    pzb = open_piece(("win", 10 + j // 4))
                jb = j % 2
                b = pe_job(mm_cm(pzb[0], j % 4, par), reads=[pzb[1]] + hn)
                P.op("act", lambda e, b=b, j=j: e.activation(out=us[:, j, :], in_=banks[b][:, :], func=AF.Silu),
                     reads=["ps%d" % b], writes=["us%d" % j])
                if j % 4 == 3:
                    close_piece(pzb[2])
                P.op("pool", lambda e, j=j, jb=jb: e.tensor_tensor(out=t32a[jb][:, :], in0=ys[:, j, :], in1=meanB[:, :],
                                                                   op=ALU.subtract),
                     reads=["ys%d" % j, "meanB"], writes=["t32a%d" % jb])
                P.op("dve", lambda e, jb=jb: e.tensor_tensor(out=t32b[jb][:, :], in0=t32a[jb][:, :], in1=rstdB[:, :],
                                                             op=ALU.mult),
                     reads=["t32a%d" % jb, "rstdB"], writes=["t32b%d" % jb])
                P.op("act", lambda e, j=j, jb=jb: e.activation(out=ys[:, j, :], in_=t32b[jb][:, :], func=AF.Silu,
                                                               bias=col(C_CLB, j), scale=col(C_CLG, j)),
                     reads=["t32b%d" % jb, "cols"], writes=["ys%d" % j])
                P.op("dve", lambda e, j=j: e.tensor_tensor(out=ys[:, j, :], in0=ys[:, j, :], in1=zs[:, j, :],
                                                           op=ALU.mult),
                     reads=["ys%d" % j, "zs%d" % j], writes=["ys%d" % j])
            for j in range(8):
                if j % 4 == 0:
                    pu = open_piece(("win", 6 + j // 4))
                jq = j % 4
                b = pe_job(mm_cm(pu[0], j % 4, par), reads=[pu[1]] + hn)
                P.op("act", lambda e, b=b, jq=jq: e.activation(out=sg[jq][:, :], in_=banks[b][:, :], func=AF.Gelu),
                     reads=["ps%d" % b], writes=["sg%d" % jq])
                P.op("dve", lambda e, j=j, jq=jq: e.tensor_tensor(out=us[:, j, :], in0=us[:, j, :], in1=sg[jq][:, :],
                                                                  op=ALU.mult),
                     reads=["us%d" % j, "sg%d" % jq], writes=["us%d" % j])
                if j % 4 == 3:
                    close_piece(pu[2])
                if i + 1 < NT and j % 2 == 1:
                    xprep_a(i + 1, j // 2)
                    if j >= 3:
                        xprep_b(i + 1, j // 2 - 1)
            if i + 1 < NT:
                xprep_b(i + 1, 3)
            if i == 0:
                dump("ya", ys, ["ys%d" % k for k in range(8)]); dump("uz", us, ["us%d" % k for k in range(8)])
                dump("meanB", meanB, ["meanB"]); dump("rstdB", rstdB, ["rstdB"]); dump("zs", zs, ["zs%d" % k for k in range(8)])
            pvl = open_piece(("win", 8))
            pvh = open_piece(("win", 9))
            for u in range(4):
                vb = v32[u % 2]
                vname = "big%d" % (8 + u % 2)
                for half, pc in ((0, pvl), (1, pvh)):
                    def vjob(pe, bank, u=u, wbuf=pc[0], par=par):
                        last = None
                        for kc in range(8):
                            last = pe.matmul(bank[:, :], hT[par][:, kc, u * 128:(u + 1) * 128], wbuf[:, kc, :],
                                             start=(kc == 0), stop=(kc == 7))
                        return last
                    b = pe_job(vjob, reads=[pc[1]] + hn)
                    P.op("act", lambda e, b=b, vb=vb, half=half: e.activation(
                        out=vb[:, half * 512:(half + 1) * 512], in_=banks[b][:, :], func=AF.Gelu),
                        reads=["ps%d" % b], writes=[vname])
                k = 2 + u % 2
                small_stats(k, vb, vname, False)
                P.op("dve", lambda e, u=u, vb=vb, k=k: e.tensor_scalar(
                    out=vnb[:, u, :], in0=vb[:, :], scalar1=mv[k][:, 0:1], scalar2=rstd1[k][:, 0:1],
                    op0=ALU.subtract, op1=ALU.mult),
                    reads=[vname, "mv_%d" % k, "rstd1_%d" % k], writes=["vn%d" % u])
            close_piece(pvl[2])
            close_piece(pvh[2])
            if i == 0:
                dump("vn", vnb, ["vn%d" % k for k in range(4)])
            if i + 1 < NT:
                xprep_transposes(i + 1)
            yn_all = ["ys%d" % k for k in range(8)]
            vn_all = ["vn%d" % k for k in range(4)]
            for j in range(8):
                if j % 4 == 0:
                    pga = open_piece(("win", 12 + j // 4))
                    pco = open_piece(("wco", j // 4))
                jb = j % 2
                lc = j % 4
                b = pe_job(mm_cm(pga[0], lc, par), reads=[pga[1]] + hn)
                P.op("act", lambda e, b=b, jb=jb: e.activation(out=sg[jb][:, :], in_=banks[b][:, :], func=AF.Sigmoid),
                     reads=["ps%d" % b], writes=["sg%d" % jb])

                def yajob(pe, bank, lc=lc, wbuf=pco[0]):
                    last = None
                    for kc in range(8):
                        last = pe.matmul(bank[:, :], wbuf[:, kc, lc * 128:(lc + 1) * 128], ys[:, kc, :],
                                         start=(kc == 0), stop=(kc == 7))
                    return last
                b2 = pe_job(yajob, reads=[pco[1]] + yn_all)
                P.op("dve", lambda e, b2=b2, j=j, jb=jb: e.tensor_tensor(out=ms[:, j, :], in0=banks[b2][:, :],
                                                                        in1=sg[jb][:, :], op=ALU.mult),
                     reads=["ps%d" % b2, "sg%d" % jb], writes=["ms%d" % j])
                if j % 4 == 3:
                    close_piece(pga[2])
                    close_piece(pco[2])
                h = j

                def sjob2(pe, bank, h=h):
                    last = None
                    for u in range(4):
                        last = pe.matmul(bank[:, u * 128:(u + 1) * 128], vnb[:, u, h * 128:(h + 1) * 128],
                                         wsTm[:, h, :], start=True, stop=True)
                    return last
                b = pe_job(sjob2, reads=vn_all + ["wsTm"])
                hb = h % 2
                P.op("dve", lambda e, b=b, h=h, hb=hb: e.scalar_tensor_tensor(
                    out=t32a[hb].ap().rearrange("p (u t) -> p u t", u=4),
                    in0=banks[b].ap().rearrange("p (u t) -> p u t", u=4),
                    scalar=col(C_SLG, h),
                    in1=Cb[:, h, :].unsqueeze(1).to_broadcast([128, 4, 128]),
                    op0=ALU.mult, op1=ALU.add),
                    reads=["ps%d" % b, "cols", "Cb"], writes=["t32a%d" % hb])
                P.op("dve", lambda e, h=h, hb=hb: e.tensor_tensor(out=us[:, h, :], in0=t32a[hb][:, :], in1=us[:, h, :],
                                                                  op=ALU.mult),
                     reads=["t32a%d" % hb, "us%d" % h], writes=["us%d" % h])
            if i == 0:
                dump("mA", ms, ["ms%d" % k for k in range(8)])
                dump("yb", us, ["us%d" % k for k in range(8)])
            def xf_load(u):
                row0 = i * T + u * 128
                k = 2 + u % 2
                P.dma("sp", xf[u % 2].ap(), x_d[row0:row0 + 128, :], writes=["big%d" % k], key="ld_big%d" % k)
            xf_load(0)
            xf_load(1)
            un_all = ["us%d" % k for k in range(8)]
            for j in range(8):
                if j % 4 == 0:
                    pgb = open_piece(("win", 14 + j // 4))
                    pso = open_piece(("wso", j // 4))
                jb = j % 2
                lc = j % 4
                b = pe_job(mm_cm(pgb[0], lc, par), reads=[pgb[1]] + hn)
                P.op("act", lambda e, b=b, jb=jb: e.activation(out=sg[jb][:, :], in_=banks[b][:, :], func=AF.Sigmoid),
                     reads=["ps%d" % b], writes=["sg%d" % jb])

                def ybjob(pe, bank, lc=lc, wbuf=pso[0]):
                    last = None
                    for kc in range(8):
                        last = pe.matmul(bank[:, :], wbuf[:, kc, lc * 128:(lc + 1) * 128], us[:, kc, :],
                                         start=(kc == 0), stop=(kc == 7))
                    return last
                b2 = pe_job(ybjob, reads=[pso[1]] + un_all)
                P.op("dve", lambda e, b2=b2, jb=jb: e.tensor_tensor(out=mbt[jb][:, :], in0=banks[b2][:, :],
                                                                   in1=sg[jb][:, :], op=ALU.mult),
                     reads=["ps%d" % b2, "sg%d" % jb], writes=["mbt%d" % jb])
                P.op("dve", lambda e, j=j, jb=jb: e.tensor_tensor(out=ms[:, j, :], in0=ms[:, j, :], in1=mbt[jb][:, :],
                                                                  op=ALU.add),
                     reads=["ms%d" % j, "mbt%d" % jb], writes=["ms%d" % j])
                if j % 4 == 3:
                    close_piece(pgb[2])
                    close_piece(pso[2])
            if i == 0:
                dump("merged", ms, ["ms%d" % k for k in range(8)])
            mn_all = ["ms%d" % k for k in range(8)]
            pol = open_piece(("wo", 0))
            poh = open_piece(("wo", 1))
            for u in range(4):
                xb = xf[u % 2]
                xname = "big%d" % (2 + u % 2)
                for half, pc in ((0, pol), (1, poh)):
                    def ojob(pe, bank, u=u, wbuf=pc[0]):
                        last = None
                        for kc in range(8):
                            last = pe.matmul(bank[:, :], ms[:, kc, u * 128:(u + 1) * 128], wbuf[:, kc, :],
                                             start=(kc == 0), stop=(kc == 7))
                        return last
                    b = pe_job(ojob, reads=[pc[1]] + mn_all)
                    P.op("dve", lambda e, b=b, half=half: e.tensor_tensor(
                        out=t32b[half][:, :], in0=banks[b][:, :], in1=gateb[:, half * 512:(half + 1) * 512],
                        op=ALU.mult),
                        reads=["ps%d" % b, "gateb"], writes=["t32b%d" % half])
                    P.op("dve", lambda e, xb=xb, half=half: e.tensor_tensor(
                        out=xb[:, half * 512:(half + 1) * 512], in0=t32b[half][:, :],
                        in1=xb[:, half * 512:(half + 1) * 512], op=ALU.add),
                        reads=["t32b%d" % half, xname], writes=[xname])

                def tail(u=u, xb=xb, xname=xname, i=i):
                    k = 4 + u % 2
                    small_stats(k, xb, xname, True)
                    for half in range(2):
                        P.op("dve", lambda e, xb=xb, half=half, k=k: e.scalar_tensor_tensor(
                            out=xb[:, half * 512:(half + 1) * 512], in0=xb[:, half * 512:(half + 1) * 512],
                            scalar=rstd1[k][:, 0:1], in1=gfin[:, half * 512:(half + 1) * 512],
                            op0=ALU.mult, op1=ALU.mult),
                            reads=[xname, "rstd1_%d" % k, "gfin"], writes=[xname])
                    row0 = i * T + u * 128
                    P.dma("sp", out_d[row0:row0 + 128, :], xb.ap(), reads=[xname], writes=["out_%d_%d" % (i, u)],
                          key="st_big%d" % (2 + u % 2))
                if u < 2:
                    tail()
                    xf_load(u + 2)
                else:
                    deferred.append(tail)
            close_piece(pol[2])
            close_piece(poh[2])
        for fn in deferred:
            fn()
        del deferred[:]

        P.wait_all("sp")
        P.emit()
    return nc


_NC_CACHE = {}


def _host_layout(inputs, b):
    f = lambda a: np.ascontiguousarray(a, dtype=np.float32)
    colT = lambda v: np.asarray(v, dtype=np.float32).reshape(8, 128).T
    cols = np.stack([colT(inputs["c"][b]), colT(inputs["g_pre"][0]), colT(inputs["conv_b"][0]),
                     colT(inputs["conv_ln_g"][0]), colT(inputs["conv_ln_b"][0]),
                     colT(inputs["sgu_ln_g"][0]), colT(inputs["sgu_ln_b"][0])], axis=1).reshape(128, 56)
    return {"x": f(inputs["x"][b]), "cols": f(cols)}


def kernel(**inputs):
    inputs = {k: np.asarray(v) for k, v in inputs.items()}
    n = 8
    f = lambda a: np.ascontiguousarray(a, dtype=np.float32)
    cw = np.concatenate([inputs["conv_w"][0], np.zeros((1, D), np.float32)], axis=0)
    q = np.arange(8)[:, None]
    s = np.arange(4)[None, :]
    kidx = 30 - 4 * q - s
    kidx = np.where(kidx < 0, 31, kidx)
    g5 = cw[kidx]
    g5 = g5.reshape(8, 4, 8, 4, 32)
    cwp = np.transpose(g5, (1, 4, 2, 3, 0)).reshape(128, 256)
    wsT = np.transpose(inputs["w_sgu"][0], (2, 0, 1)).reshape(128, 1024)
    bs_bc = np.broadcast_to(inputs["b_sgu"][0].reshape(1, 1024), (128, 1024))
    bada_bc = np.broadcast_to(inputs["b_ada"][0].reshape(1, 3 * D), (128, 3 * D))
    gfin_bc = np.broadcast_to(inputs["g_final"].reshape(1, D), (128, D))
    ident = np.eye(128, dtype=np.float32)
    maskT = (np.arange(128)[:, None] <= np.arange(128)[None, :]).astype(np.float32)
    E = np.tile(np.eye(32, dtype=np.float32), (4, 1))
    consts = np.concatenate([ident, maskT, E], axis=1)
    shared = {
        "w_ada": f(inputs["w_ada"][0]), "b_ada_bc": f(bada_bc), "w_in": f(inputs["w_in"][0]),
        "cwp": f(cwp), "w_conv_out": f(inputs["w_conv_out"][0]), "w_sgu_out": f(inputs["w_sgu_out"][0]),
        "w_o": f(inputs["w_o"][0]), "wsT": f(wsT), "bs_bc": f(bs_bc), "gfin_bc": f(gfin_bc),
        "consts": f(consts),
    }
    in_maps = []
    for b in range(n):
        m = dict(shared)
        m.update(_host_layout(inputs, b))
        in_maps.append(m)
    if "nc" not in _NC_CACHE:
        _NC_CACHE["nc"] = build_nc()
    nc = _NC_CACHE["nc"]
    res = run_bass_kernel_spmd(nc, in_maps, core_ids=list(range(n)))
    out = np.stack([np.asarray(res.results[b]["out"], dtype=np.float32) for b in range(n)], axis=0)
    return out
```

```python
import numpy as np
import concourse.bass as bass
import concourse.mybir as mybir
from concourse.bass_utils import run_bass_kernel_spmd
from contextlib import ExitStack

F32 = mybir.dt.float32
BF16 = mybir.dt.bfloat16
ALU = mybir.AluOpType
AF = mybir.ActivationFunctionType

ENG = ("pe", "act", "dve", "pool", "sp")

D = 1024
S = 4096
T = 512
NT = S // T
EPS = 1e-6
AH = 32
RW = 540
RH = 28


class Prog:
    def __init__(self, nc, stack):
        self.nc = nc
        self.items = {e: [] for e in ENG}
        self.esem = {e: stack.enter_context(nc.semaphore("s_" + e)) for e in ENG if e != "sp"}
        self.ecnt = {e: 0 for e in ENG}
        self.seen = {e: {} for e in ENG}
        self.dsem = {}
        self.stack = stack
        self.buf = {}

    def _st(self, b):
        st = self.buf.get(b)
        if st is None:
            st = {"w": None, "r": {}}
            self.buf[b] = st
        return st

    def _deps(self, e, reads, writes):
        deps = []
        for b in reads:
            st = self._st(b)
            if st["w"] is not None:
                deps.append((st["w"], True))
        for b in writes:
            st = self._st(b)
            if st["w"] is not None:
                deps.append((st["w"], False))
            for t in st["r"].values():
                deps.append((t, False))
        need = {}
        for (sem, val, src), raw in deps:
            if src == e:
                if e == "pe":
                    continue
                if e in ("act", "dve", "pool") and not raw:
                    continue
            k = id(sem)
            if val > need.get(k, (None, 0))[1]:
                need[k] = (sem, val)
        for k, (sem, val) in need.items():
            if self.seen[e].get(k, 0) >= val:
                continue
            self.seen[e][k] = val
            self.items[e].append(("wait", sem, val))

    def _mark(self, tok, reads, writes):
        for b in reads:
            st = self._st(b)
            k = id(tok[0])
            old = st["r"].get(k)
            if old is None or old[1] < tok[1]:
                st["r"][k] = tok
        for b in writes:
            st = self._st(b)
            st["w"] = tok
            st["r"] = {}

    def op(self, e, fn, reads=(), writes=()):
        self._deps(e, reads, writes)
        self.ecnt[e] += 1
        tok = (self.esem[e], self.ecnt[e], e)
        self.items[e].append(("op", fn, self.esem[e], 1))
        self._mark(tok, reads, writes)
        return tok

    def dma(self, q, out, in_, reads=(), writes=(), key=None, **kw):
        self._deps(q, reads, writes)
        if key not in self.dsem:
            self.dsem[key] = [self.stack.enter_context(self.nc.semaphore("d_" + str(key))), 0]
        ent = self.dsem[key]
        ent[1] += 16
        tok = (ent[0], ent[1], "dma:" + str(key))

        def fn(eng, out=out, in_=in_, kw=kw):
            return eng.dma_start(out=out, in_=in_, **kw)

        self.items[q].append(("op", fn, ent[0], 16))
        self._mark(tok, reads, writes)
        return tok

    def retag(self, names, tok):
        for b in names:
            self._st(b)["w"] = tok

    def wait_all(self, e):
        need = {}
        for st in self.buf.values():
            toks = list(st["r"].values())
            if st["w"] is not None:
                toks.append(st["w"])
            for sem, val, src in toks:
                k = id(sem)
                if val > need.get(k, (None, 0))[1]:
                    need[k] = (sem, val)
        for k, (sem, val) in need.items():
            if self.seen[e].get(k, 0) >= val:
                continue
            self.seen[e][k] = val
            self.items[e].append(("wait", sem, val))

    def emit(self):
        nc = self.nc
        items = self.items

        def replay(name, eng):
            for it in items[name]:
                if it[0] == "wait":
                    eng.wait_ge(it[1], it[2])
                else:
                    ins = it[1](eng)
                    ins.then_inc(it[2], it[3])

        with nc.Block() as block:
            @block.tensor
            def _(pe):
                replay("pe", pe)

            @block.scalar
            def _(act):
                replay("act", act)

            @block.vector
            def _(dve):
                replay("dve", dve)

            @block.gpsimd
            def _(pool):
                replay("pool", pool)

            @block.sync
            def _(sp):
                replay("sp", sp)


C_C, C_GPRE, C_CONVB, C_CLG, C_CLB, C_SLG, C_SLB = range(7)


def build_nc(NT=NT, debug=False):
    nc = bass.Bass("TRN2", target_bir_lowering=False)
    dbg_n = [0]
    dt_in = lambda name, shape: nc.dram_tensor(name, shape, F32, kind="ExternalInput").ap()
    x_d = dt_in("x", [S, D])
    wada_d = dt_in("w_ada", [D, 3 * D])
    bada_d = dt_in("b_ada_bc", [128, 3 * D])
    win_d = dt_in("w_in", [D, 8 * D])
    cwp_d = dt_in("cwp", [128, 256])
    cols_d = dt_in("cols", [128, 56])
    wco_d = dt_in("w_conv_out", [D, D])
    wso_d = dt_in("w_sgu_out", [D, D])
    wo_d = dt_in("w_o", [D, D])
    wsT_d = dt_in("wsT", [128, 1024])
    bs_d = dt_in("bs_bc", [128, 1024])
    gfin_d = dt_in("gfin_bc", [128, 1024])
    cst_d = dt_in("consts", [128, 288])
    out_d = nc.dram_tensor("out", [S, D], F32, kind="ExternalOutput").ap()
    scr_win = nc.dram_tensor("scr_win", [D, 8 * D], BF16).ap()
    scr_wco = nc.dram_tensor("scr_wco", [D, D], BF16).ap()
    scr_wso = nc.dram_tensor("scr_wso", [D, D], BF16).ap()
    scr_wo = nc.dram_tensor("scr_wo", [D, D], BF16).ap()

    with ExitStack() as stk:
        P = Prog(nc, stk)
        sb = lambda name, shape, dtype: nc.alloc_sbuf_tensor("sb_" + name, shape, dtype)
        cst = sb("cst", [128, 288], F32)
        ident32 = cst[:, 0:128]
        maskT = cst[:, 128:256]
        Emask = cst[:, 256:288]
        ones_bf = sb("ones_bf", [128, 128], BF16)
        ones32 = sb("ones32", [128, 128], F32)
        neghalf = sb("neghalf", [128, 512], F32)
        cols = sb("cols", [128, 56], F32)
        a_col = sb("a_col", [128, 8], F32)
        shift_col = sb("shift_col", [128, 8], F32)
        cwp = sb("cwp", [128, 256], F32)
        Wk = sb("Wk", [128, 8, 4, 8, 32], BF16)
        wsTm = sb("wsTm", [128, 8, 128], BF16)
        Cb = sb("Cb", [128, 8, 128], F32)
        gfin = sb("gfin", [128, 1024], F32)
        big = [None, None] + [sb("big%d" % i, [128, 1024], F32) for i in range(2, 10)]
        vnb = sb("vn", [128, 4, 1024], BF16)
        xf = big[2:4]
        xn = big[4:8]
        v32 = big[8:10]
        hT = [sb("hT%d" % i, [128, 8, T], BF16) for i in range(2)]
        aT = sb("aT", [128, 8, AH + T], BF16)
        Rb = [sb("R%d" % i, [128, 2, 4, RW], BF16) for i in range(2)]
        ys = sb("ys", [128, 8, T], BF16)
        zs = sb("zs", [128, 8, T], BF16)
        us = sb("us", [128, 8, T], BF16)
        ms = sb("ms", [128, 8, T], BF16)
        sg = [sb("sg%d" % i, [128, T], BF16) for i in range(4)]
        ysq = [sb("ysq%d" % i, [128, T], BF16) for i in range(2)]
        mbt = [sb("mbt%d" % i, [128, T], BF16) for i in range(2)]
        t32a = [sb("t32a%d" % i, [128, T], F32) for i in range(2)]
        t32b = [sb("t32b%d" % i, [128, T], F32) for i in range(2)]
        meanB = sb("meanB", [128, T], F32)
        mbt32 = sb("mbt32", [128, T], F32)
        msqB = sb("msqB", [128, T], F32)
        rstdB = sb("rstdB", [128, T], F32)
        wb = [sb("wb%d" % i, [128, 8, T], BF16) for i in range(4)]
        gateb = sb("gateb", [128, 1024], F32)
        NSM = 10
        st6 = [sb("st6_%d" % i, [128, 12], F32) for i in range(NSM)]
        mv = [sb("mv_%d" % i, [128, 2], F32) for i in range(NSM)]
        tmp1 = [sb("tmp1_%d" % i, [128, 1], F32) for i in range(NSM)]
        rstd1 = [sb("rstd1_%d" % i, [128, 1], F32) for i in range(NSM)]
        banks = [nc.alloc_psum_tensor("ps%d" % i, [128, 512], F32) for i in range(8)]
        NRING = 6
        ring = [0]

        def dump(tag, t, names):
            if not debug:
                return
            shp = list(t.shape)
            flat = [shp[0], int(np.prod(shp[1:]))]
            d = nc.dram_tensor("dbg_" + tag, flat, t.dtype, kind="ExternalOutput").ap()
            a = t.ap()
            if len(shp) == 3:
                a = a.rearrange("p a b -> p (a b)")
            elif len(shp) == 5:
                a = a.rearrange("p a b c d -> p (a b c d)")
            P.dma("sp", d, a, reads=names, writes=["dbgout_" + tag], key="dbg")

        def col(k, j):
            return cols[:, k * 8 + j:k * 8 + j + 1]

        def nextbank():
            b = ring[0] % NRING
            ring[0] += 1
            return b

        def pe_job(fn, reads, b=None):
            if b is None:
                b = nextbank()
            P.op("pe", lambda pe, fn=fn, b=b: fn(pe, banks[b]), reads=reads, writes=["ps%d" % b])
            return b

        P.dma("sp", cst.ap(), cst_d, writes=["cst"], key="c0")
        P.dma("sp", cols.ap(), cols_d, writes=["cols"], key="c0")
        P.dma("sp", cwp.ap(), cwp_d, writes=["cwp"], key="c0")
        tokc = P.dma("sp", gfin.ap(), gfin_d, writes=["gfin"], key="c0")
        P.retag(["cst", "cols", "cwp", "gfin"], tokc)
        P.op("dve", lambda e: e.memset(ones_bf.ap(), 1.0), writes=["ones_bf"])
        P.op("dve", lambda e: e.memset(ones32.ap(), 1.0), writes=["ones32"])
        P.op("dve", lambda e: e.memset(neghalf.ap(), -0.5), writes=["neghalf"])
        P.op("dve", lambda e: e.memset(aT.ap(), 0.0), writes=["aT%d" % j for j in range(8)] + ["aTh%d" % j for j in range(8)])

        piece_src = {}
        cast_done = set()
        for cb in range(16):
            piece_src[("win", cb)] = (scr_win, cb * T, ["scr_win%d" % cb])
        for h in range(2):
            piece_src[("wco", h)] = (scr_wco, h * T, ["scr_wco"])
            piece_src[("wso", h)] = (scr_wso, h * T, ["scr_wso"])
            piece_src[("wo", h)] = (scr_wo, h * T, ["scr_wo"])

        def ensure_cast(kind):
            k0 = kind[0]
            tag = kind if k0 == "win" else (k0,)
            if tag in cast_done:
                return
            cast_done.add(tag)
            if k0 == "win":
                cb = kind[1]
                P.dma("pool", scr_win[:, cb * T:(cb + 1) * T], win_d[:, cb * T:(cb + 1) * T],
                      reads=([] if cb in (2, 0) else ["a_col"]),
                      writes=["scr_win%d" % cb], key="cast%d" % (len(cast_done) % 6))
            elif k0 == "wco":
                P.dma("pool", scr_wco, wco_d, reads=["a_col"], writes=["scr_wco"], key="cast%d" % (len(cast_done) % 6))
            elif k0 == "wso":
                P.dma("pool", scr_wso, wso_d, reads=["a_col"], writes=["scr_wso"], key="cast%d" % (len(cast_done) % 6))
            else:
                P.dma("pool", scr_wo, wo_d, reads=["a_col"], writes=["scr_wo"], key="cast%d" % (len(cast_done) % 6))

        for kind in (("win", 2), ("win", 0)):
            ensure_cast(kind)

        tile_pieces = [("win", 2), ("win", 0), ("win", 3), ("win", 1), ("win", 4), ("win", 5),
                       ("win", 10), ("win", 11), ("win", 6), ("win", 7),
                       ("win", 8), ("win", 9),
                       ("win", 12), ("wco", 0), ("win", 13), ("wco", 1),
                       ("win", 14), ("wso", 0), ("win", 15), ("wso", 1), ("wo", 0), ("wo", 1)]
        all_pieces = []
        for i in range(NT):
            all_pieces += tile_pieces
        issued = [0]
        opened = [0]

        def issue_upto(m):
            while issued[0] < min(m, len(all_pieces)):
                n = issued[0]
                for la in range(n, min(n + 5, len(all_pieces))):
                    ensure_cast(all_pieces[la])
                src, c0, snames = piece_src[all_pieces[n]]
                k = n % 4
                P.dma("sp", wb[k].ap(), src.rearrange("(kc p) n -> p kc n", p=128)[:, :, c0:c0 + T],
                      reads=snames, writes=["wb%d" % k], key="wb%d" % k)
                issued[0] += 1

        def open_piece(kind):
            n = opened[0]
            assert all_pieces[n] == kind, (all_pieces[n], kind)
            opened[0] += 1
            issue_upto(n + 1)
            return wb[n % 4], "wb%d" % (n % 4), n

        def close_piece(n):
            issue_upto(n + 4 + 1)

        def small_stats(k, src, sname, rms):
            P.op("dve", lambda e: e.bn_stats(out=st6[k][:, 0:6], in_=src[:, 0:512]), reads=[sname], writes=["st6a_%d" % k])
            P.op("dve", lambda e: e.bn_stats(out=st6[k][:, 6:12], in_=src[:, 512:1024]), reads=[sname], writes=["st6b_%d" % k])
            P.op("dve", lambda e: e.bn_aggr(out=mv[k][:, :], in_=st6[k][:, :]),
                 reads=["st6a_%d" % k, "st6b_%d" % k], writes=["mv_%d" % k])
            if rms:
                P.op("dve", lambda e: e.scalar_tensor_tensor(
                    out=tmp1[k][:, :], in0=mv[k][:, 0:1], scalar=mv[k][:, 0:1], in1=mv[k][:, 1:2],
                    op0=ALU.mult, op1=ALU.add), reads=["mv_%d" % k], writes=["tmp1_%d" % k])
                P.op("dve", lambda e: e.tensor_scalar(out=tmp1[k][:, :], in0=tmp1[k][:, :], scalar1=EPS, scalar2=None,
                                                      op0=ALU.add), reads=["tmp1_%d" % k], writes=["tmp1_%d" % k])
            else:
                P.op("dve", lambda e: e.tensor_scalar(out=tmp1[k][:, :], in0=mv[k][:, 1:2], scalar1=EPS, scalar2=None,
                                                      op0=ALU.add), reads=["mv_%d" % k], writes=["tmp1_%d" % k])
            P.op("pool", lambda e: e.tensor_tensor(out=rstd1[k][:, :], in0=tmp1[k][:, :], in1=neghalf[:, 0:1],
                                                   op=ALU.pow),
                 reads=["tmp1_%d" % k, "neghalf"], writes=["rstd1_%d" % k])

        def xprep_a(i, u):
            row0 = i * T + u * 128
            nm = "big%d" % (4 + u)
            P.dma("sp", xn[u].ap(), x_d[row0:row0 + 128, :], writes=[nm], key="ld_" + nm)
            small_stats(6 + u, xn[u], nm, True)

        def xprep_b(i, u):
            k = 6 + u
            nm = "big%d" % (4 + u)
            P.op("dve", lambda e: e.tensor_scalar(out=xn[u][:, :], in0=xn[u][:, :], scalar1=rstd1[k][:, 0:1],
                                                  scalar2=None, op0=ALU.mult),
                 reads=[nm, "rstd1_%d" % k], writes=[nm])

        def xprep_transposes(i):
            par = i % 2
            for kc in range(8):
                def tjob(pe, bank, kc=kc):
                    last = None
                    for u in range(4):
                        last = pe.transpose(out=bank[:, u * 128:(u + 1) * 128],
                                            in_=xn[u][:, kc * 128:(kc + 1) * 128], identity=ident32)
                    return last
                b = pe_job(tjob, reads=["big4", "big5", "big6", "big7", "cst"])
                P.op("act", lambda e, b=b, kc=kc: e.activation(
                    out=hT[par][:, kc, :], in_=banks[b][:, :], func=AF.Identity,
                    bias=shift_col[:, kc:kc + 1], scale=a_col[:, kc:kc + 1]),
                    reads=["ps%d" % b, "shift_col", "a_col"], writes=["hT%d_%d" % (par, kc)])

        def hT_names(par):
            return ["hT%d_%d" % (par, kc) for kc in range(8)]

        def mm_cm(wbuf, lc, par):
            def fn(pe, bank):
                last = None
                for kc in range(8):
                    last = pe.matmul(bank[:, :], wbuf[:, kc, lc * 128:(lc + 1) * 128], hT[par][:, kc, :],
                                     start=(kc == 0), stop=(kc == 7))
                return last
            return fn

        for u in range(4):
            xprep_a(0, u)

        t512 = [(t32a[0], "t32a0"), (t32a[1], "t32a1"), (t32b[0], "t32b0"), (t32b[1], "t32b1")]
        tb = [0]

        def next_t512():
            r = t512[tb[0] % len(t512)]
            tb[0] += 1
            return r

        t512b = [(meanB, "meanB"), (msqB, "msqB"), (rstdB, "rstdB"), (mbt32, "mbt32")]
        tb2 = [0]

        def next_t512b():
            r = t512b[tb2[0] % len(t512b)]
            tb2[0] += 1
            return r

        def ada_block(acc, aname, cb, queue="sp", split=False):
            for kc in range(8):
                for half in range(2):
                    sl = slice(half * 512, (half + 1) * 512)
                    buf, bname = next_t512b() if split else next_t512()
                    c0 = cb * 1024 + half * 512
                    P.dma(queue, buf.ap(), wada_d[kc * 128:(kc + 1) * 128, c0:c0 + 512],
                          writes=[bname], key="ld_" + bname)
                    if split:
                        if kc == 0:
                            P.op("act", lambda e, buf=buf, sl=sl: e.activation(
                                out=acc[:, sl], in_=buf[:, :], func=AF.Identity, scale=col(C_C, 0)),
                                reads=[bname, "cols"], writes=[aname])
                        else:
                            P.op("act", lambda e, buf=buf, kc=kc: e.activation(
                                out=buf[:, :], in_=buf[:, :], func=AF.Identity, scale=col(C_C, kc)),
                                reads=[bname, "cols"], writes=[bname])
                            P.op("pool", lambda e, buf=buf, sl=sl: e.tensor_tensor(
                                out=acc[:, sl], in0=acc[:, sl], in1=buf[:, :], op=ALU.add),
                                reads=[bname, aname], writes=[aname])
                    elif kc == 0:
                        P.op("dve", lambda e, buf=buf, sl=sl: e.tensor_scalar(
                            out=acc[:, sl], in0=buf[:, :], scalar1=col(C_C, 0), scalar2=None, op0=ALU.mult),
                            reads=[bname, "cols"], writes=[aname])
                    else:
                        P.op("dve", lambda e, buf=buf, sl=sl, kc=kc: e.scalar_tensor_tensor(
                            out=acc[:, sl], in0=buf[:, :], scalar=col(C_C, kc), in1=acc[:, sl],
                            op0=ALU.mult, op1=ALU.add),
                            reads=[bname, "cols", aname], writes=[aname])
            for half in range(2):
                sl = slice(half * 512, (half + 1) * 512)
                buf, bname = next_t512()
                c0 = cb * 1024 + half * 512
                P.dma(queue, buf.ap(), bada_d[:, c0:c0 + 512], writes=[bname], key="ld_" + bname)
                b = pe_job(lambda pe, bank, sl=sl: pe.matmul(bank[:, :], ones32[:, :], acc[:, sl],
                                                             start=True, stop=True),
                           reads=["ones32", aname])
                P.op("dve", lambda e, b=b, sl=sl, buf=buf: e.tensor_tensor(
                    out=acc[:, sl], in0=banks[b][:, :], in1=buf[:, :], op=ALU.add),
                    reads=["ps%d" % b, bname], writes=[aname])

        accs = [(v32[0], "big8"), (v32[1], "big9")]
        for kc in range(8):
            for cb in range(2):
                acc, aname = accs[cb]
                for half in range(2):
                    sl = slice(half * 512, (half + 1) * 512)
                    buf, bname = next_t512()
                    c0 = cb * 1024 + half * 512
                    P.dma("sp", buf.ap(), wada_d[kc * 128:(kc + 1) * 128, c0:c0 + 512], writes=[bname], key="ld_" + bname)
                    if kc == 0:
                        P.op("dve", lambda e, buf=buf, sl=sl, acc=acc: e.tensor_scalar(
                            out=acc[:, sl], in0=buf[:, :], scalar1=col(C_C, 0), scalar2=None, op0=ALU.mult),
                            reads=[bname, "cols"], writes=[aname])
                    else:
                        P.op("dve", lambda e, buf=buf, sl=sl, kc=kc, acc=acc: e.scalar_tensor_tensor(
                            out=acc[:, sl], in0=buf[:, :], scalar=col(C_C, kc), in1=acc[:, sl],
                            op0=ALU.mult, op1=ALU.add),
                            reads=[bname, "cols", aname], writes=[aname])
        for cb in range(2):
            acc, aname = accs[cb]
            for half in range(2):
                sl = slice(half * 512, (half + 1) * 512)
                buf, bname = next_t512()
                c0 = cb * 1024 + half * 512
                P.dma("sp", buf.ap(), bada_d[:, c0:c0 + 512], writes=[bname], key="ld_" + bname)
                b = pe_job(lambda pe, bank, sl=sl, acc=acc: pe.matmul(bank[:, :], ones32[:, :], acc[:, sl],
                                                                      start=True, stop=True),
                           reads=["ones32", aname])
                P.op("dve", lambda e, b=b, sl=sl, buf=buf, acc=acc: e.tensor_tensor(
                    out=acc[:, sl], in0=banks[b][:, :], in1=buf[:, :], op=ALU.add),
                    reads=["ps%d" % b, bname], writes=[aname])
        gslots = []
        for slot, nm, per in ((zs, "zs", 2), (us, "us", 2), (ms, "ms", 2)):
            v = slot.ap().rearrange("p a b -> p (a b)").bitcast(F32)
            for q in range(4):
                gslots.append((v[:, q * 512:(q + 1) * 512], ["%s%d" % (nm, 2 * q), "%s%d" % (nm, 2 * q + 1)]))
        v = vnb.ap().rearrange("p a b -> p (a b)").bitcast(F32)
        for q in range(4):
            gslots.append((v[:, q * 512:(q + 1) * 512], ["vn%d" % q]))
        gi = 0
        gate_ops = []
        issue_upto(2)
        gtok = None
        gall = []
        for kc in range(8):
            for half in range(2):
                sl = slice(half * 512, (half + 1) * 512)
                gap_, gnames = gslots[gi]
                gi += 1
                c0 = 2 * 1024 + half * 512
                gtok = P.dma("sp", gap_, wada_d[kc * 128:(kc + 1) * 128, c0:c0 + 512], writes=gnames, key="ld_gate")
                gate_ops.append((kc, sl, gap_, gnames))
                gall += gnames
        P.retag(gall, gtok)
        P.dma("sp", big[2].ap(), bada_d[:, 2048:3072], writes=["big2"], key="ld_big2")

        def gate_acc(lo, hi):
            def fn():
                for kc, sl, gap_, gnames in gate_ops[lo:hi]:
                    if kc == 0:
                        P.op("dve", lambda e, sl=sl, gap_=gap_: e.tensor_scalar(
                            out=gateb[:, sl], in0=gap_, scalar1=col(C_C, 0), scalar2=None, op0=ALU.mult),
                            reads=gnames + ["cols"], writes=["gateb"])
                    else:
                        P.op("dve", lambda e, sl=sl, gap_=gap_, kc=kc: e.scalar_tensor_tensor(
                            out=gateb[:, sl], in0=gap_, scalar=col(C_C, kc), in1=gateb[:, sl],
                            op0=ALU.mult, op1=ALU.add),
                            reads=gnames + ["cols", "gateb"], writes=["gateb"])
            return fn

        def gate_finalize():
            for half in range(2):
                sl = slice(half * 512, (half + 1) * 512)
                b = pe_job(lambda pe, bank, sl=sl: pe.matmul(bank[:, :], ones32[:, :], gateb[:, sl],
                                                             start=True, stop=True),
                           reads=["ones32", "gateb"])
                P.op("dve", lambda e, b=b, sl=sl: e.tensor_tensor(
                    out=gateb[:, sl], in0=banks[b][:, :], in1=big[2][:, sl], op=ALU.add),
                    reads=["ps%d" % b, "big2"], writes=["gateb"])

        for u in range(4):
            xprep_b(0, u)
        cwp5 = cwp.ap().rearrange("p (j g q) -> p j g q", j=8, g=4)
        for j in range(8):
            for g in range(4):
                P.op("pool", lambda e, j=j, g=g: e.tensor_tensor(
                    out=Wk[:, j, g, :, :],
                    in0=Emask.unsqueeze(1).to_broadcast([128, 8, 32]),
                    in1=cwp5[:, j, g, :].unsqueeze(2).to_broadcast([128, 8, 32]),
                    op=ALU.mult), reads=["cst", "cwp"], writes=["Wk"])

        def colx(pe, bank):
            last = None
            for idx in range(16):
                src = v32[idx // 8]
                kc = idx % 8
                last = pe.matmul(bank[:, 2 * idx:2 * idx + 2], src[:, kc * 128:(kc + 1) * 128], ident32[:, 0:2],
                                 start=True, stop=True)
            return last
        b = pe_job(colx, reads=["big8", "big9", "cst"])
        P.op("dve", lambda e, b=b: e.tensor_copy(out=shift_col[:, :], in_=banks[b][:, 0:16:2]),
             reads=["ps%d" % b], writes=["shift_col"])
        P.op("dve", lambda e, b=b: e.scalar_tensor_tensor(
            out=a_col[:, :], in0=banks[b][:, 16:32:2], scalar=1.0, in1=cols[:, C_GPRE * 8:C_GPRE * 8 + 8],
            op0=ALU.add, op1=ALU.mult), reads=["ps%d" % b, "cols"], writes=["a_col"])
        dump("a_col", a_col, ["a_col"]); dump("shift_col", shift_col, ["shift_col"])
        issue_upto(4)
        xprep_transposes(0)

        P.dma("sp", big[8].ap(), wsT_d, writes=["big8"], key="ld_big8")
        P.op("dve", lambda e: e.tensor_tensor(
            out=wsTm[:, :, :], in0=big[8].ap().rearrange("p (h t) -> p h t", h=8),
            in1=maskT.unsqueeze(1).to_broadcast([128, 8, 128]), op=ALU.mult),
            reads=["big8", "cst"], writes=["wsTm"])
        P.dma("sp", big[3].ap(), bs_d, writes=["big3"], key="ld_big3")
        wsflat = wsTm.ap().rearrange("p h t -> p (h t)")
        for half in range(2):
            b = pe_job(lambda pe, bank, half=half: pe.matmul(bank[:, :], ones_bf[:, :],
                                                             wsflat[:, half * 512:(half + 1) * 512],
                                                             start=True, stop=True),
                       reads=["ones_bf", "wsTm"])
            for hh in range(4):
                h = half * 4 + hh
                P.op("dve", lambda e, b=b, h=h, hh=hh: e.scalar_tensor_tensor(
                    out=Cb[:, h, :], in0=banks[b][:, hh * 128:(hh + 1) * 128], scalar=col(C_SLB, h),
                    in1=big[3][:, h * 128:(h + 1) * 128], op0=ALU.mult, op1=ALU.add),
                    reads=["ps%d" % b, "cols", "big3"], writes=["Cb"])
        dump("Cb", Cb, ["Cb"]); dump("wsTm", wsTm, ["wsTm"]); dump("Wk", Wk, ["Wk"])

        dump("hT0", hT[0], hT_names(0))
        deferred = [gate_acc(0, 4), gate_acc(4, 8), gate_acc(8, 12), gate_acc(12, 16), gate_finalize]
        for i in range(NT):
            par = i % 2
            hn = hT_names(par)
            def conv(j):
                pb = (j // 2) % 2
                jj = j % 2
                jb = j % 2
                rn = ["R%d_%d_%d" % (pb, g, s) for g in range(4) for s in range(4)]

                def cjob(pe, bank):
                    last = None
                    for q in range(8):
                        for g in range(4):
                            last = pe.matmul(bank[32 * g:32 * g + 32, :], Wk[:, j, g, q, :],
                                             Rb[pb][:, jj, g, RH - 4 * q:RH - 4 * q + T],
                                             start=(q == 0), stop=(q == 7), tile_position=(0, 32 * g))
                    return last
                b = pe_job(cjob, reads=rn + ["Wk"])
                P.op("act", lambda e, b=b: e.activation(out=ys[:, j, :], in_=banks[b][:, :], func=AF.Identity,
                                                        bias=col(C_CONVB, j), scale=1.0),
                     reads=["ps%d" % b, "cols"], writes=["ys%d" % j])
                P.op("act", lambda e, b=b: e.activation(out=ysq[jb][:, :], in_=banks[b][:, :], func=AF.Square,
                                                        bias=col(C_CONVB, j), scale=1.0),
                     reads=["ps%d" % b, "cols"], writes=["ysq%d" % jb])

            def stats(j):
                jb = j % 2

                def sjob(pe):
                    pe.matmul(banks[6][:, :], ones_bf[:, :], ys[:, j, :], start=(j == 0), stop=(j == 7))
                    return pe.matmul(banks[7][:, :], ones_bf[:, :], ysq[jb][:, :], start=(j == 0), stop=(j == 7))
                P.op("pe", sjob, reads=["ones_bf", "ys%d" % j, "ysq%d" % jb], writes=["ps6", "ps7"])

            def glu_val(j, pg, pv):
                lc = j % 4
                jb = j % 2
                b1 = pe_job(mm_cm(pg[0], lc, par), reads=[pg[1]] + hn)
                P.op("act", lambda e, b1=b1, jb=jb: e.activation(out=sg[jb][:, :], in_=banks[b1][:, :],
                                                                 func=AF.Sigmoid),
                     reads=["ps%d" % b1], writes=["sg%d" % jb])
                b2 = pe_job(mm_cm(pv[0], lc, par), reads=[pv[1]] + hn)
                P.op("dve", lambda e, b2=b2, jb=jb, j=j: e.tensor_tensor(
                    out=aT[:, j, AH:AH + T], in0=banks[b2][:, :], in1=sg[jb][:, :], op=ALU.mult),
                    reads=["ps%d" % b2, "sg%d" % jb], writes=["aT%d" % j])
                if j >= 1 and deferred:
                    deferred.pop(0)()

            def replicas(p):
                pb = p % 2
                j0 = 2 * p
                for q, key, gs in (("sp", "R%d" % pb, (0, 1)), ("pool", "RP%d" % pb, (2, 3))):
                    rn = []
                    tok = None
                    for g in gs:
                        for s in range(4):
                            nm = "R%d_%d_%d" % (pb, g, s)
                            rn.append(nm)
                            c0 = AH - RH - s
                            tok = P.dma(q, Rb[pb][32 * s:32 * s + 32, :, g, :],
                                        aT[32 * g:32 * g + 32, j0:j0 + 2, c0:c0 + RW],
                                        reads=["aT%d" % j0, "aTh%d" % j0, "aT%d" % (j0 + 1), "aTh%d" % (j0 + 1)],
                                        writes=[nm], key=key)
                    P.retag(rn, tok)

            def zA_job(j, pz):
                b = pe_job(mm_cm(pz[0], j % 4, par), reads=[pz[1]] + hn)
                P.op("act", lambda e, b=b, j=j: e.activation(out=zs[:, j, :], in_=banks[b][:, :], func=AF.Silu),
                     reads=["ps%d" % b], writes=["zs%d" % j])

            pg = open_piece(("win", 2))
            pv = open_piece(("win", 0))
            glu_val(0, pg, pv); glu_val(1, pg, pv); replicas(0)
            glu_val(2, pg, pv); glu_val(3, pg, pv); replicas(1)
            close_piece(pg[2]); close_piece(pv[2])
            pg = open_piece(("win", 3))
            pv = open_piece(("win", 1))
            glu_val(4, pg, pv)
            conv(0); conv(1)
            glu_val(5, pg, pv); replicas(2)
            glu_val(6, pg, pv)
            stats(0); stats(1); conv(2); conv(3)
            glu_val(7, pg, pv); replicas(3)
            close_piece(pg[2]); close_piece(pv[2])
            while deferred:
                deferred.pop(0)()
            pz = open_piece(("win", 4))
            zA_job(0, pz)
            stats(2); stats(3); conv(4); conv(5)
            zA_job(1, pz); zA_job(2, pz); zA_job(3, pz)
            close_piece(pz[2])
            pz = open_piece(("win", 5))
            zA_job(4, pz)
            stats(4); stats(5); conv(6); conv(7)
            zA_job(5, pz); zA_job(6, pz)
            stats(6); stats(7)
            zA_job(7, pz)
            close_piece(pz[2])
            for j in range(8):
                P.op("pool", lambda e, j=j: e.tensor_copy(out=aT[:, j, 0:AH], in_=aT[:, j, T:T + AH]),
                     reads=["aT%d" % j], writes=["aTh%d" % j])
            if i == 0:
                dump("aT", aT, ["aT%d" % k for k in range(8)] + ["aTh%d" % k for k in range(8)])
                dump("yconv", ys, ["ys%d" % k for k in range(8)])
            P.op("dve", lambda e: e.tensor_scalar(out=meanB[:, :], in0=banks[6][:, :], scalar1=1.0 / D, scalar2=None,
                                                  op0=ALU.mult), reads=["ps6"], writes=["meanB"])
            P.op("dve", lambda e: e.tensor_tensor(out=msqB[:, :], in0=meanB[:, :], in1=meanB[:, :], op=ALU.mult),
                 reads=["meanB"], writes=["msqB"])
            P.op("dve", lambda e: e.scalar_tensor_tensor(out=rstdB[:, :], in0=banks[7][:, :], scalar=1.0 / D,
                                                         in1=msqB[:, :], op0=ALU.mult, op1=ALU.subtract),
                 reads=["ps7", "msqB"], writes=["rstdB"])
            P.op("dve", lambda e: e.tensor_scalar(out=rstdB[:, :], in0=rstdB[:, :], scalar1=0.0, scalar2=EPS,
                                                  op0=ALU.max, op1=ALU.add), reads=["rstdB"], writes=["rstdB"])
            P.op("act", lambda e: e.activation(out=msqB[:, :], in_=rstdB[:, :], func=AF.Sqrt),
                 reads=["rstdB"], writes=["msqB"])
            P.op("dve", lambda e: e.reciprocal(out=rstdB[:, :], in_=msqB[:, :]), reads=["msqB"], writes=["rstdB"])
            for j in range(8):
                if j % 4 == 0:
                    pzb = open_piece(("win", 10 + j // 4))
                jb = j % 2
                b = pe_job(mm_cm(pzb[0], j % 4, par), reads=[pzb[1]] + hn)
                P.op("act", lambda e, b=b, j=j: e.activation(out=us[:, j, :], in_=banks[b][:, :], func=AF.Silu),
                     reads=["ps%d" % b], writes=["us%d" % j])
                if j % 4 == 3:
                    close_piece(pzb[2])
                P.op("pool", lambda e, j=j, jb=jb: e.tensor_tensor(out=t32a[jb][:, :], in0=ys[:, j, :], in1=meanB[:, :],
                                                                   op=ALU.subtract),
                     reads=["ys%d" % j, "meanB"], writes=["t32a%d" % jb])
                P.op("dve", lambda e, jb=jb: e.tensor_tensor(out=t32b[jb][:, :], in0=t32a[jb][:, :], in1=rstdB[:, :],
                                                             op=ALU.mult),
                     reads=["t32a%d" % jb, "rstdB"], writes=["t32b%d" % jb])
                P.op("act", lambda e, j=j, jb=jb: e.activation(out=ys[:, j, :], in_=t32b[jb][:, :], func=AF.Silu,
                                                               bias=col(C_CLB, j), scale=col(C_CLG, j)),
                     reads=["t32b%d" % jb, "cols"], writes=["ys%d" % j])
                P.op("dve", lambda e, j=j: e.tensor_tensor(out=ys[:, j, :], in0=ys[:, j, :], in1=zs[:, j, :],
                                                           op=ALU.mult),
                     reads=["ys%d" % j, "zs%d" % j], writes=["ys%d" % j])
            for j in range(8):
                if j % 4 == 0:
                    pu = open_piece(("win", 6 + j // 4))
                jq = j % 4
                b = pe_job(mm_cm(pu[0], j % 4, par), reads=[pu[1]] + hn)
                P.op("act", lambda e, b=b, jq=jq: e.activation(out=sg[jq][:, :], in_=banks[b][:, :], func=AF.Gelu),
                     reads=["ps%d" % b], writes=["sg%d" % jq])
                P.op("dve", lambda e, j=j, jq=jq: e.tensor_tensor(out=us[:, j, :], in0=us[:, j, :], in1=sg[jq][:, :],
                                                                  op=ALU.mult),
                     reads=["us%d" % j, "sg%d" % jq], writes=["us%d" % j])
                if j % 4 == 3:
                    close_piece(pu[2])
                if i + 1 < NT and j % 2 == 1:
                    xprep_a(i + 1, j // 2)
                    if j >= 3:
                        xprep_b(i + 1, j // 2 - 1)
            if i + 1 < NT:
                xprep_b(i + 1, 3)
            if i == 0:
                dump("ya", ys, ["ys%d" % k for k in range(8)]); dump("uz", us, ["us%d" % k for k in range(8)])
                dump("meanB", meanB, ["meanB"]); dump("rstdB", rstdB, ["rstdB"]); dump("zs", zs, ["zs%d" % k for k in range(8)])
            pvl = open_piece(("win", 8))
            pvh = open_piece(("win", 9))
            for u in range(4):
                vb = v32[u % 2]
                vname = "big%d" % (8 + u % 2)
                for half, pc in ((0, pvl), (1, pvh)):
                    def vjob(pe, bank, u=u, wbuf=pc[0], par=par):
                        last = None
                        for kc in range(8):
                            last = pe.matmul(bank[:, :], hT[par][:, kc, u * 128:(u + 1) * 128], wbuf[:, kc, :],
                                             start=(kc == 0), stop=(kc == 7))
                        return last
                    b = pe_job(vjob, reads=[pc[1]] + hn)
                    P.op("act", lambda e, b=b, vb=vb, half=half: e.activation(
                        out=vb[:, half * 512:(half + 1) * 512], in_=banks[b][:, :], func=AF.Gelu),
                        reads=["ps%d" % b], writes=[vname])
                k = 2 + u % 2
                small_stats(k, vb, vname, False)
                P.op("dve", lambda e, u=u, vb=vb, k=k: e.tensor_scalar(
                    out=vnb[:, u, :], in0=vb[:, :], scalar1=mv[k][:, 0:1], scalar2=rstd1[k][:, 0:1],
                    op0=ALU.subtract, op1=ALU.mult),
                    reads=[vname, "mv_%d" % k, "rstd1_%d" % k], writes=["vn%d" % u])
            close_piece(pvl[2])
            close_piece(pvh[2])
            if i == 0:
                dump("vn", vnb, ["vn%d" % k for k in range(4)])
            if i + 1 < NT:
                xprep_transposes(i + 1)
            yn_all = ["ys%d" % k for k in range(8)]
            vn_all = ["vn%d" % k for k in range(4)]
            for j in range(8):
                if j % 4 == 0:
                    pga = open_piece(("win", 12 + j // 4))
                    pco = open_piece(("wco", j // 4))
                jb = j % 2
                lc = j % 4
                b = pe_job(mm_cm(pga[0], lc, par), reads=[pga[1]] + hn)
                P.op("act", lambda e, b=b, jb=jb: e.activation(out=sg[jb][:, :], in_=banks[b][:, :], func=AF.Sigmoid),
                     reads=["ps%d" % b], writes=["sg%d" % jb])

                def yajob(pe, bank, lc=lc, wbuf=pco[0]):
                    last = None
                    for kc in range(8):
                        last = pe.matmul(bank[:, :], wbuf[:, kc, lc * 128:(lc + 1) * 128], ys[:, kc, :],
                                         start=(kc == 0), stop=(kc == 7))
                    return last
                b2 = pe_job(yajob, reads=[pco[1]] + yn_all)
                P.op("dve", lambda e, b2=b2, j=j, jb=jb: e.tensor_tensor(out=ms[:, j, :], in0=banks[b2][:, :],
                                                                        in1=sg[jb][:, :], op=ALU.mult),
                     reads=["ps%d" % b2, "sg%d" % jb], writes=["ms%d" % j])
                if j % 4 == 3:
                    close_piece(pga[2])
                    close_piece(pco[2])
                h = j

                def sjob2(pe, bank, h=h):
                    last = None
                    for u in range(4):
                        last = pe.matmul(bank[:, u * 128:(u + 1) * 128], vnb[:, u, h * 128:(h + 1) * 128],
                                         wsTm[:, h, :], start=True, stop=True)
                    return last
                b = pe_job(sjob2, reads=vn_all + ["wsTm"])
                hb = h % 2
                P.op("dve", lambda e, b=b, h=h, hb=hb: e.scalar_tensor_tensor(
                    out=t32a[hb].ap().rearrange("p (u t) -> p u t", u=4),
                    in0=banks[b].ap().rearrange("p (u t) -> p u t", u=4),
                    scalar=col(C_SLG, h),
                    in1=Cb[:, h, :].unsqueeze(1).to_broadcast([128, 4, 128]),
                    op0=ALU.mult, op1=ALU.add),
                    reads=["ps%d" % b, "cols", "Cb"], writes=["t32a%d" % hb])
                P.op("dve", lambda e, h=h, hb=hb: e.tensor_tensor(out=us[:, h, :], in0=t32a[hb][:, :], in1=us[:, h, :],
                                                                  op=ALU.mult),
                     reads=["t32a%d" % hb, "us%d" % h], writes=["us%d" % h])
            if i == 0:
                dump("mA", ms, ["ms%d" % k for k in range(8)])
                dump("yb", us, ["us%d" % k for k in range(8)])
            def xf_load(u):
                row0 = i * T + u * 128
                k = 2 + u % 2
                P.dma("sp", xf[u % 2].ap(), x_d[row0:row0 + 128, :], writes=["big%d" % k], key="ld_big%d" % k)
            xf_load(0)
            xf_load(1)
            un_all = ["us%d" % k for k in range(8)]
            for j in range(8):
                if j % 4 == 0:
                    pgb = open_piece(("win", 14 + j // 4))
                    pso = open_piece(("wso", j // 4))
                jb = j % 2
                lc = j % 4
                b = pe_job(mm_cm(pgb[0], lc, par), reads=[pgb[1]] + hn)
                P.op("act", lambda e, b=b, jb=jb: e.activation(out=sg[jb][:, :], in_=banks[b][:, :], func=AF.Sigmoid),
                     reads=["ps%d" % b], writes=["sg%d" % jb])

                def ybjob(pe, bank, lc=lc, wbuf=pso[0]):
                    last = None
                    for kc in range(8):
                        last = pe.matmul(bank[:, :], wbuf[:, kc, lc * 128:(lc + 1) * 128], us[:, kc, :],
                                         start=(kc == 0), stop=(kc == 7))
                    return last
                b2 = pe_job(ybjob, reads=[pso[1]] + un_all)
                P.op("dve", lambda e, b2=b2, jb=jb: e.tensor_tensor(out=mbt[jb][:, :], in0=banks[b2][:, :],
                                                                   in1=sg[jb][:, :], op=ALU.mult),
                     reads=["ps%d" % b2, "sg%d" % jb], writes=["mbt%d" % jb])
                P.op("dve", lambda e, j=j, jb=jb: e.tensor_tensor(out=ms[:, j, :], in0=ms[:, j, :], in1=mbt[jb][:, :],
                                                                  op=ALU.add),
                     reads=["ms%d" % j, "mbt%d" % jb], writes=["ms%d" % j])
                if j % 4 == 3:
                    close_piece(pgb[2])
                    close_piece(pso[2])
            if i == 0:
                dump("merged", ms, ["ms%d" % k for k in range(8)])
            mn_all = ["ms%d" % k for k in range(8)]
            pol = open_piece(("wo", 0))
            poh = open_piece(("wo", 1))
            for u in range(4):
                xb = xf[u % 2]
                xname = "big%d" % (2 + u % 2)
                for half, pc in ((0, pol), (1, poh)):
                    def ojob(pe, bank, u=u, wbuf=pc[0]):
                        last = None
                        for kc in range(8):
                            last = pe.matmul(bank[:, :], ms[:, kc, u * 128:(u + 1) * 128], wbuf[:, kc, :],
                                             start=(kc == 0), stop=(kc == 7))
                        return last
                    b = pe_job(ojob, reads=[pc[1]] + mn_all)
                    P.op("dve", lambda e, b=b, half=half: e.tensor_tensor(
                        out=t32b[half][:, :], in0=banks[b][:, :], in1=gateb[:, half * 512:(half + 1) * 512],
                        op=ALU.mult),
                        reads=["ps%d" % b, "gateb"], writes=["t32b%d" % half])
                    P.op("dve", lambda e, xb=xb, half=half: e.tensor_tensor(
                        out=xb[:, half * 512:(half + 1) * 512], in0=t32b[half][:, :],
                        in1=xb[:, half * 512:(half + 1) * 512], op=ALU.add),
                        reads=["t32b%d" % half, xname], writes=[xname])

                def tail_a(u=u, xb=xb, xname=xname):
                    small_stats(4 + u % 2, xb, xname, True)

                def tail_b(u=u, xb=xb, xname=xname, i=i):
                    k = 4 + u % 2
                    for half in range(2):
                        P.op("dve", lambda e, xb=xb, half=half, k=k: e.scalar_tensor_tensor(
                            out=xb[:, half * 512:(half + 1) * 512], in0=xb[:, half * 512:(half + 1) * 512],
                            scalar=rstd1[k][:, 0:1], in1=gfin[:, half * 512:(half + 1) * 512],
                            op0=ALU.mult, op1=ALU.mult),
                            reads=[xname, "rstd1_%d" % k, "gfin"], writes=[xname])
                    row0 = i * T + u * 128
                    P.dma("sp", out_d[row0:row0 + 128, :], xb.ap(), reads=[xname], writes=["out_%d_%d" % (i, u)],
                          key="st_big%d" % (2 + u % 2))
                if u < 2:
                    tail_a()
                    tail_b()
                    xf_load(u + 2)
                else:
                    deferred.append(tail_a)
                    deferred.append(tail_b)
            close_piece(pol[2])
            close_piece(poh[2])
        while deferred:
            deferred.pop(0)()

        P.wait_all("sp")
        P.emit()
    return nc


_NC_CACHE = {}


def _host_layout(inputs, b):
    f = lambda a: np.ascontiguousarray(a, dtype=np.float32)
    colT = lambda v: np.asarray(v, dtype=np.float32).reshape(8, 128).T
    cols = np.stack([colT(inputs["c"][b]), colT(inputs["g_pre"][0]), colT(inputs["conv_b"][0]),
                     colT(inputs["conv_ln_g"][0]), colT(inputs["conv_ln_b"][0]),
                     colT(inputs["sgu_ln_g"][0]), colT(inputs["sgu_ln_b"][0])], axis=1).reshape(128, 56)
    return {"x": f(inputs["x"][b]), "cols": f(cols)}


def kernel(**inputs):
    inputs = {k: np.asarray(v) for k, v in inputs.items()}
    n = 8
    f = lambda a: np.ascontiguousarray(a, dtype=np.float32)
    cw = np.concatenate([inputs["conv_w"][0], np.zeros((1, D), np.float32)], axis=0)
    q = np.arange(8)[:, None]
    s = np.arange(4)[None, :]
    kidx = 30 - 4 * q - s
    kidx = np.where(kidx < 0, 31, kidx)
    g5 = cw[kidx]
    g5 = g5.reshape(8, 4, 8, 4, 32)
    cwp = np.transpose(g5, (1, 4, 2, 3, 0)).reshape(128, 256)
    wsT = np.transpose(inputs["w_sgu"][0], (2, 0, 1)).reshape(128, 1024)
    bs_bc = np.broadcast_to(inputs["b_sgu"][0].reshape(1, 1024), (128, 1024))
    bada_bc = np.broadcast_to(inputs["b_ada"][0].reshape(1, 3 * D), (128, 3 * D))
    gfin_bc = np.broadcast_to(inputs["g_final"].reshape(1, D), (128, D))
    ident = np.eye(128, dtype=np.float32)
    maskT = (np.arange(128)[:, None] <= np.arange(128)[None, :]).astype(np.float32)
    E = np.tile(np.eye(32, dtype=np.float32), (4, 1))
    consts = np.concatenate([ident, maskT, E], axis=1)
    shared = {
        "w_ada": f(inputs["w_ada"][0]), "b_ada_bc": f(bada_bc), "w_in": f(inputs["w_in"][0]),
        "cwp": f(cwp), "w_conv_out": f(inputs["w_conv_out"][0]), "w_sgu_out": f(inputs["w_sgu_out"][0]),
        "w_o": f(inputs["w_o"][0]), "wsT": f(wsT), "bs_bc": f(bs_bc), "gfin_bc": f(gfin_bc),
        "consts": f(consts),
    }
    in_maps = []
    for b in range(n):
        m = dict(shared)
        m.update(_host_layout(inputs, b))
        in_maps.append(m)
    if "nc" not in _NC_CACHE:
        _NC_CACHE["nc"] = build_nc()
    nc = _NC_CACHE["nc"]
    res = run_bass_kernel_spmd(nc, in_maps, core_ids=list(range(n)))
    out = np.stack([np.asarray(res.results[b]["out"], dtype=np.float32) for b in range(n)], axis=0)
    return out
```
